# Optimizing a Trainium2 kernel written in Bass

```python
import jax, jax.numpy as jnp
from jax import lax
import numpy as np

D_MODEL = 2048
BATCH = 2
SEQ = 16384
DEPTH = 2

N_MIXERS = 2
RET_HEADS = 8
RET_QK_DIM = D_MODEL // RET_HEADS
RET_V_WIDTH = 2 * D_MODEL
RET_V_DIM = RET_V_WIDTH // RET_HEADS
CHUNK = 128
ROPE_BASE = 10000.0
CONV_WIDTH = 31
FFN_HIDDEN = -(-8 * D_MODEL // (3 * 256)) * 256
PLE_DIM = 256
N_RET = (DEPTH + 1) // 2
N_CONV = DEPTH // 2
EPS = 1e-6

kernel_name = "hybrid_retention_conformer_block"


def rms_norm(x, g):
    xf = x.astype(jnp.float32)
    y = xf * lax.rsqrt(jnp.mean(xf * xf, axis=-1, keepdims=True) + EPS)
    return (y * g.astype(jnp.float32)).astype(x.dtype)


def layer_norm(x, g, b):
    xf = x.astype(jnp.float32)
    xc = xf - jnp.mean(xf, axis=-1, keepdims=True)
    var = jnp.mean(xc * xc, axis=-1, keepdims=True)
    y = xc * lax.rsqrt(var + EPS) * g.astype(jnp.float32) + b.astype(jnp.float32)
    return y.astype(x.dtype)


def rotary(t, positions):
    half = t.shape[-1] // 2
    inv_freq = ROPE_BASE ** (-jnp.arange(half, dtype=jnp.float32) / half)
    ang = positions.astype(jnp.float32)[..., None] * inv_freq
    cos = jnp.cos(ang)[:, :, None, :]
    sin = jnp.sin(ang)[:, :, None, :]
    t1, t2 = t[..., :half], t[..., half:]
    return jnp.concatenate([t1 * cos - t2 * sin, t1 * sin + t2 * cos], axis=-1)


def chunkwise_retention(q, k, v):
    b, s, h, dk = q.shape
    dv = v.shape[-1]
    n_chunks = s // CHUNK
    log_gamma = jnp.log1p(-jnp.exp2(-5.0 - jnp.arange(RET_HEADS, dtype=jnp.float32)))
    idx = jnp.arange(CHUNK, dtype=jnp.float32)
    rel = idx[:, None] - idx[None, :]
    intra = jnp.where(rel >= 0, jnp.exp(log_gamma[:, None, None] * jnp.maximum(rel, 0.0)), 0.0)
    q_decay = jnp.exp(log_gamma[:, None] * (idx + 1.0))[..., None]
    k_decay = jnp.exp(log_gamma[:, None] * (CHUNK - 1.0 - idx))[..., None]
    state_decay = jnp.exp(log_gamma * CHUNK)[:, None, None]

    def to_chunks(t):
        return t.reshape(b, n_chunks, CHUNK, h, t.shape[-1]).transpose(1, 0, 3, 2, 4)

    def step(state, qkv):
        qc, kc, vc = qkv
        scores = jnp.einsum('bhid,bhjd->bhij', qc, kc) * intra
        out = (jnp.einsum('bhij,bhjv->bhiv', scores, vc)
               + jnp.einsum('bhid,bhdv->bhiv', qc, state) * q_decay)
        state = state * state_decay + jnp.einsum('bhjd,bhjv->bhdv', kc * k_decay, vc)
        return state, out

    state0 = jnp.zeros((b, h, dk, dv), jnp.float32)
    _, out = lax.scan(step, state0, (to_chunks(q), to_chunks(k), to_chunks(v)))
    return out.transpose(1, 0, 3, 2, 4).reshape(b, s, h, dv)


def retention_mixer(hn, w_in, w_out, positions):
    b, s, _ = hn.shape
    proj = (hn @ w_in).astype(jnp.float32)
    q, k, v, g = jnp.split(proj, [D_MODEL, 2 * D_MODEL, 2 * D_MODEL + RET_V_WIDTH], axis=-1)
    q = rotary(q.reshape(b, s, RET_HEADS, RET_QK_DIM), positions)
    k = rotary(k.reshape(b, s, RET_HEADS, RET_QK_DIM), positions) * (RET_QK_DIM ** -0.5)
    v = v.reshape(b, s, RET_HEADS, RET_V_DIM)
    o = chunkwise_retention(q, k, v)
    o = o * lax.rsqrt(jnp.mean(o * o, axis=-1, keepdims=True) + EPS)
    y = (jax.nn.silu(g) * o.reshape(b, s, RET_V_WIDTH)).astype(hn.dtype)
    return y @ w_out


def conformer_conv(hn, w_pw1, b_pw1, w_dw, b_dw, ln_g, ln_b, w_pw2, b_pw2):
    a = hn @ w_pw1 + b_pw1
    u = a[..., :D_MODEL] * jax.nn.sigmoid(a[..., D_MODEL:])
    u = lax.conv_general_dilated(
        u, w_dw[:, None, :], window_strides=(1,), padding=[(CONV_WIDTH - 1, 0)],
        dimension_numbers=('NWC', 'WIO', 'NWC'), feature_group_count=D_MODEL) + b_dw
    u = jax.nn.silu(layer_norm(u, ln_g, ln_b))
    return u @ w_pw2 + b_pw2


def swiglu(hn, w_gate, w_up, w_down):
    return (jax.nn.silu(hn @ w_gate) * (hn @ w_up)) @ w_down


def setup_inputs(seed: int = 0) -> dict:
    key = jax.random.key(seed)
    ks = jax.random.split(key, 24)

    def nrm(k, shape, scale):
        return scale * jax.random.normal(k, shape, jnp.float32)

    def gain(k, shape):
        return 1.0 + nrm(k, shape, 0.05)

    D, F = D_MODEL, FFN_HIDDEN
    return {
        'x': nrm(ks[0], (BATCH, SEQ, D), 1.0),
        'p': nrm(ks[1], (DEPTH, BATCH, SEQ, PLE_DIM), 1.0),
        'positions': jnp.broadcast_to(jnp.arange(SEQ, dtype=jnp.int32)[None, :], (BATCH, SEQ)),
        'ret_w_in': nrm(ks[2], (N_RET, D, 2 * D + 2 * RET_V_WIDTH), D ** -0.5),
        'ret_w_out': nrm(ks[3], (N_RET, RET_V_WIDTH, D), RET_V_WIDTH ** -0.5),
        'conv_w_pw1': nrm(ks[4], (N_CONV, D, 2 * D), D ** -0.5),
        'conv_b_pw1': nrm(ks[5], (N_CONV, 2 * D), 0.02),
        'conv_w_dw': nrm(ks[6], (N_CONV, CONV_WIDTH, D), CONV_WIDTH ** -0.5),
        'conv_b_dw': nrm(ks[7], (N_CONV, D), 0.02),
        'conv_ln_g': gain(ks[8], (N_CONV, D)),
        'conv_ln_b': nrm(ks[9], (N_CONV, D), 0.02),
        'conv_w_pw2': nrm(ks[10], (N_CONV, D, D), D ** -0.5),
        'conv_b_pw2': nrm(ks[11], (N_CONV, D), 0.02),
        'g_mix_pre': gain(ks[12], (DEPTH, D)),
        'g_mix_post': gain(ks[13], (DEPTH, D)),
        'ffn_w_gate': nrm(ks[14], (DEPTH, D, F), D ** -0.5),
        'ffn_w_up': nrm(ks[15], (DEPTH, D, F), D ** -0.5),
        'ffn_w_down': nrm(ks[16], (DEPTH, F, D), F ** -0.5),
        'g_ffn_pre': gain(ks[17], (DEPTH, D)),
        'g_ffn_post': gain(ks[18], (DEPTH, D)),
        'ple_w_proj': nrm(ks[19], (DEPTH, PLE_DIM, D), PLE_DIM ** -0.5),
        'ple_w_gate': nrm(ks[20], (DEPTH, D, D), D ** -0.5),
        'g_ple': gain(ks[21], (DEPTH, D)),
    }


def reference(x, p, positions, ret_w_in, ret_w_out, conv_w_pw1, conv_b_pw1, conv_w_dw,
              conv_b_dw, conv_ln_g, conv_ln_b, conv_w_pw2, conv_b_pw2, g_mix_pre, g_mix_post,
              ffn_w_gate, ffn_w_up, ffn_w_down, g_ffn_pre, g_ffn_post, ple_w_proj, ple_w_gate,
              g_ple):
    h = x
    for i in range(DEPTH):
        j = i // N_MIXERS
        hn = rms_norm(h, g_mix_pre[i])
        if i % N_MIXERS == 0:
            y = retention_mixer(hn, ret_w_in[j], ret_w_out[j], positions)
        else:
            y = conformer_conv(hn, conv_w_pw1[j], conv_b_pw1[j], conv_w_dw[j], conv_b_dw[j],
                               conv_ln_g[j], conv_ln_b[j], conv_w_pw2[j], conv_b_pw2[j])
        h = h + rms_norm(y, g_mix_post[i])
        f = swiglu(rms_norm(h, g_ffn_pre[i]), ffn_w_gate[i], ffn_w_up[i], ffn_w_down[i])
        h = h + rms_norm(f, g_ffn_post[i])
        gate = jax.nn.sigmoid(h @ ple_w_gate[i])
        h = h + rms_norm(gate * (p[i] @ ple_w_proj[i]), g_ple[i])
    return h
```

```python
import math
from contextlib import ExitStack

import numpy as np
import concourse.bass as bass
import concourse.mybir as mybir
from concourse.bass_utils import run_bass_kernel_spmd

F32 = mybir.dt.float32
BF16 = mybir.dt.bfloat16
I32 = mybir.dt.int32
ACT = mybir.ActivationFunctionType
ALU = mybir.AluOpType

D = 2048
NC16 = 16
HEADS = 8
DK = 256
DV = 512
VW = 4096
FF = 5632
NF = 44
PLE = 256
CW = 31
EPS = 1e-6
TT = 512
NSLOT = 4
SLOT = 4096
PIECE = 32768
TWO_PI = 2.0 * math.pi
C1 = 6.28125
C2 = TWO_PI - C1


class Buf:
    __slots__ = ("last_w", "readers")

    def __init__(self):
        self.last_w = None
        self.readers = []


class DSem:
    __slots__ = ("handle", "count")

    def __init__(self, handle):
        self.handle = handle
        self.count = 0


class Op:
    __slots__ = ("eng", "fn", "deps", "signal", "tok", "dsem", "is_dma", "seq", "inc")

    def __init__(self, eng, fn, dsem, seq, inc=16):
        self.inc = inc
        self.eng = eng
        self.fn = fn
        self.deps = []
        self.signal = False
        self.tok = None
        self.dsem = dsem
        self.is_dma = dsem is not None
        self.seq = seq


class Sched:
    ENGS = ("pe", "act", "dve", "pool", "sp")
    ENGOBJ = {"pe": "tensor", "act": "scalar", "dve": "vector", "pool": "gpsimd", "sp": "sync"}

    def __init__(self, nc):
        self.nc = nc
        self.ops = {e: [] for e in self.ENGS}
        self.nops = 0
        self.nwaits = 0

    @staticmethod
    def _add_dep(deps, d):
        if not d.is_dma:
            for i, x in enumerate(deps):
                if (not x.is_dma) and x.eng == d.eng:
                    if d.seq > x.seq:
                        deps[i] = d
                    return
        else:
            for x in deps:
                if x is d:
                    return
        deps.append(d)

    def op(self, eng, fn, reads=(), writes=(), dsem=None, inc=16):
        o = Op(eng, fn, dsem, self.nops, inc)
        self.nops += 1
        deps = o.deps
        for b in reads:
            if b.last_w is not None:
                self._add_dep(deps, b.last_w)
        for b in writes:
            if b.last_w is not None:
                self._add_dep(deps, b.last_w)
            for r in b.readers:
                self._add_dep(deps, r)
        for b in reads:
            self._add_dep(b.readers, o)
        for b in writes:
            b.last_w = o
            b.readers = []
        self.ops[eng].append(o)
        return o

    def dma(self, q, out, in_, reads, writes, dsem):
        return self.op(q, lambda e: e.dma_start(out=out, in_=in_), reads, writes, dsem)

    def mm(self, out, lhsT, rhs, start, stop, reads, writes):
        return self.op("pe", lambda e: e.matmul(out, lhsT=lhsT, rhs=rhs, start=start, stop=stop), reads, writes)

    def tr(self, out, in_, ident, reads, writes):
        return self.op("pe", lambda e: e.transpose(out, in_, ident), reads, writes)

    def act(self, out, in_, func, reads, writes, bias=None, scale=None, accum_out=None):
        kw = {}
        if bias is not None:
            kw["bias"] = bias
        if scale is not None:
            kw["scale"] = scale
        if accum_out is not None:
            kw["accum_out"] = accum_out
        return self.op("act", lambda e: e.activation(out=out, in_=in_, func=func, **kw), reads, writes)

    def copy(self, eng, out, in_, reads, writes):
        if eng == "act":
            return self.act(out, in_, ACT.Copy, reads, writes)
        return self.op(eng, lambda e: e.tensor_copy(out=out, in_=in_), reads, writes)

    def tt(self, eng, out, in0, in1, op, reads, writes):
        return self.op(eng, lambda e: e.tensor_tensor(out=out, in0=in0, in1=in1, op=op), reads, writes)

    def ts(self, eng, out, in0, s1, s2, op0, op1, reads, writes):
        if op1 is None:
            return self.op(eng, lambda e: e.tensor_scalar(out=out, in0=in0, scalar1=s1, scalar2=None, op0=op0),
                           reads, writes)
        return self.op(eng, lambda e: e.tensor_scalar(out=out, in0=in0, scalar1=s1, scalar2=s2, op0=op0, op1=op1),
                       reads, writes)

    def stt(self, eng, out, in0, scalar, in1, op0, op1, reads, writes):
        return self.op(eng, lambda e: e.scalar_tensor_tensor(out=out, in0=in0, scalar=scalar, in1=in1,
                                                             op0=op0, op1=op1), reads, writes)

    def emit(self, es, final_eng, final_ops):
        nc = self.nc
        for e in self.ENGS:
            for o in self.ops[e]:
                nd = []
                for d in o.deps:
                    if (not d.is_dma) and (not o.is_dma) and d.eng == "pe" and o.eng == "pe":
                        continue
                    d.signal = True
                    nd.append(d)
                o.deps = nd
        for o in final_ops:
            o.signal = True
        for e in self.ENGS:
            cnt = 0
            for o in self.ops[e]:
                if o.is_dma:
                    o.dsem.count += o.inc
                    o.tok = (o.dsem.handle, o.dsem.count)
                elif o.signal:
                    cnt += 1
                    o.tok = (e, cnt)
        esem = {e: es.enter_context(nc.semaphore("es_" + e)) for e in ("pe", "act", "dve", "pool")}

        def run_engine(e, eng):
            waited = {}

            def wait_for(deps):
                need = {}
                for d in deps:
                    s, v = d.tok
                    if need.get(s, 0) < v:
                        need[s] = v
                for s, v in need.items():
                    if waited.get(s, 0) < v:
                        waited[s] = v
                        eng.wait_ge(esem[s] if isinstance(s, str) else s, v)
                        self.nwaits += 1

            for o in self.ops[e]:
                wait_for(o.deps)
                ins = o.fn(eng)
                if o.is_dma:
                    ins.then_inc(o.dsem.handle, o.inc)
                elif o.signal:
                    ins.then_inc(esem[e], 1)
            if e == final_eng:
                wait_for(final_ops)

        with nc.Block() as block:
            for e in self.ENGS:
                if not self.ops[e] and e != final_eng:
                    continue
                getattr(block, self.ENGOBJ[e])(lambda eng, e=e: run_engine(e, eng))


class Region:
    def __init__(self, nc, es, name, nbytes, gran=1024):
        assert nbytes % 4 == 0
        self.t = es.enter_context(nc.sbuf_tensor(name, [128, nbytes // 4], F32))
        self.nbytes = nbytes
        self.gran = gran
        self.g = [Buf() for _ in range((nbytes + gran - 1) // gran)]

    def bufs(self, lo, hi):
        return self.g[lo // self.gran:(hi - 1) // self.gran + 1]

    def f32(self, lo, n):
        return self.t[:, lo // 4: lo // 4 + n]

    def bf16(self, lo, n):
        return self.t[:, lo // 4: lo // 4 + (n + 1) // 2].bitcast(BF16)[:, 0:n]


def weight_blocks():
    blks = []

    def add(key, src, li, r0, kc, cols):
        n = sum(c[1] for c in cols)
        blks.append(dict(key=key, src=src, li=li, r0=r0, kc=kc, cols=cols, ncols=n, nel=kc * n))

    for hh in range(HEADS):
        add(("q", hh), "ret_w_in", 0, 0, 16, [(hh * DK, DK)])
        add(("k", hh), "ret_w_in", 0, 0, 16, [(D + hh * DK, DK)])
        for half in range(2):
            add(("v", hh, half), "ret_w_in", 0, half * 1024, 8, [(2 * D + hh * DV, DV)])
        for half in range(2):
            add(("g", hh, half), "ret_w_in", 0, half * 1024, 8, [(2 * D + VW + hh * DV, DV)])
    for m in range(NC16):
        add(("wo", m), "ret_w_out", 0, 0, 32, [(m * 128, 128)])
    for L in range(2):
        if L == 1:
            for m in range(NC16):
                add(("pw1", m), "conv_w_pw1", 0, 0, 16, [(m * 128, 128), (D + m * 128, 128)])
            for mb in range(8):
                add(("pw2", mb), "conv_w_pw2", 0, 0, 16, [(mb * 256, 256)])
        for fb in range(NF // 2):
            add(("fg", L, fb), "ffn_w_gate", L, 0, 16, [(fb * 256, 256)])
            add(("fu", L, fb), "ffn_w_up", L, 0, 16, [(fb * 256, 256)])
        for m in range(NC16):
            for half in range(2):
                add(("fd", L, m, half), "ffn_w_down", L, half * 22 * 128, 22, [(m * 128, 128)])
        for mb in range(8):
            add(("pg", L, mb), "ple_w_gate", L, 0, 16, [(mb * 256, 256)])
            add(("pp", L, mb), "ple_w_proj", L, 0, 2, [(mb * 256, 256)])

    off = 0
    for b in blks:
        b["off"] = off
        off += b["nel"]
    return blks, off


def cast_pieces(blks):
    pieces = []
    cur0, cur = 0, 0
    for b in blks:
        if cur + b["nel"] > PIECE and cur > 0:
            pieces.append((cur0, cur))
            cur0 += cur
            cur = 0
        b["piece"] = len(pieces)
        cur += b["nel"]
    pieces.append((cur0, cur))
    return pieces


def pack_weights(inputs, blks, total):
    ws = np.empty((128, total), np.float32)
    for b in blks:
        W = inputs[b["src"]][b["li"]]
        rows = W[b["r0"]: b["r0"] + b["kc"] * 128]
        sub = np.concatenate([rows[:, c0:c0 + n] for (c0, n) in b["cols"]], axis=1)
        sub = sub.reshape(b["kc"], 128, b["ncols"]).transpose(1, 0, 2).reshape(128, b["nel"])
        ws[:, b["off"]: b["off"] + b["nel"]] = sub
    return ws


def vec_layout():
    lay = {}
    off = 0

    def add(name, n):
        nonlocal off
        lay[name] = off
        off += n

    for L in range(2):
        for nm in ("g_mix_pre", "g_mix_post", "g_ffn_pre", "g_ffn_post", "g_ple"):
            add((nm, L), 16)
    add("b_pw1", 32)
    add("b_dw", 16)
    add("ln_g", 16)
    add("ln_b", 16)
    add("b_pw2", 16)
    add("w_dw", 16 * CW)
    add("inv_freq", 1)
    add("halo", 1)
    add("kdec", 8)
    add("g1", 8)
    add("g2s", 8)
    add("coef", 64)
    add("eps", 1)
    return lay, off


def chunked(v):
    return np.asarray(v, np.float32).reshape(-1, 128).T


def pack_vecs(inputs, core_j, seg, core_b=0, nseg=4):
    lay, n = vec_layout()
    V = np.zeros((128, n), np.float32)
    for L in range(2):
        for nm in ("g_mix_pre", "g_mix_post", "g_ffn_pre", "g_ffn_post", "g_ple"):
            V[:, lay[(nm, L)]: lay[(nm, L)] + 16] = chunked(inputs[nm][L])
    V[:, lay["b_pw1"]: lay["b_pw1"] + 32] = chunked(inputs["conv_b_pw1"][0])
    V[:, lay["b_dw"]: lay["b_dw"] + 16] = chunked(inputs["conv_b_dw"][0])
    V[:, lay["ln_g"]: lay["ln_g"] + 16] = chunked(inputs["conv_ln_g"][0])
    V[:, lay["ln_b"]: lay["ln_b"] + 16] = chunked(inputs["conv_ln_b"][0])
    V[:, lay["b_pw2"]: lay["b_pw2"] + 16] = chunked(inputs["conv_b_pw2"][0])
    wd = inputs["conv_w_dw"][0]
    for m in range(16):
        V[:, lay["w_dw"] + m * CW: lay["w_dw"] + (m + 1) * CW] = wd[:, m * 128:(m + 1) * 128].T
    half = 128
    inv_freq = (np.float32(10000.0) ** (-np.arange(half, dtype=np.float32) / np.float32(half))).astype(np.float32)
    V[:, lay["inv_freq"]] = inv_freq
    V[:, lay["halo"]] = 0.0 if core_j == 0 else 1.0
    lg = np.log1p(-np.exp2(-5.0 - np.arange(HEADS, dtype=np.float32))).astype(np.float32)
    idx = np.arange(128, dtype=np.float32)
    V[:, lay["kdec"]: lay["kdec"] + 8] = np.exp(lg[None, :] * (127.0 - idx)[:, None])
    lg64 = lg.astype(np.float64)
    V[:, lay["g1"]: lay["g1"] + 8] = np.exp(lg64[None, :] * (idx.astype(np.float64) + 1.0)[:, None])
    V[:, lay["g2s"]: lay["g2s"] + 8] = np.exp(2.0 * lg64[None, :] * (idx.astype(np.float64) + 1.0)[:, None]) / DV
    for r in range(8):
        rb, i = r // nseg, r % nseg
        for hh in range(HEADS):
            c = 0.0
            if rb == core_b and i < core_j:
                c = math.exp(float(np.float64(lg[hh])) * seg * (core_j - 1 - i))
            V[:, lay["coef"] + r * 8 + hh] = c
    V[:, lay["eps"]] = EPS
    return V


def pack_tabs():
    lg = np.log1p(-np.exp2(-5.0 - np.arange(HEADS, dtype=np.float32))).astype(np.float64)
    idx = np.arange(128, dtype=np.float64)
    T = np.zeros((128, 1, HEADS, 128), np.float32)
    for hh in range(HEADS):
        causal = (idx[None, :] >= idx[:, None]).astype(np.float64)
        T[:, 0, hh, :] = np.exp(-lg[hh] * (idx[:, None] + 1.0)) * causal
    return T


def state_decay():
    lg = np.log1p(-np.exp2(-5.0 - np.arange(HEADS, dtype=np.float32))).astype(np.float64)
    return [float(np.exp(lg[h] * 128.0)) for h in range(HEADS)]


def build_program(nmain, phases, debug=False):
    NTOK = 128 + nmain * TT
    blks, WTOT = weight_blocks()
    pieces = cast_pieces(blks)
    bidx = {b["key"]: b for b in blks}
    lay, NV = vec_layout()
    sdec = state_decay()
    fused = False
    NPRE = 3 * nmain * TT

    nc = bass.Bass("TRN2", target_bir_lowering=False)
    xT = nc.dram_tensor("xT", [16, 128, NTOK], F32, kind="ExternalInput").ap()
    pT = nc.dram_tensor("pT", [2, 2, 128, NTOK], F32, kind="ExternalInput").ap()
    posb = nc.dram_tensor("posb", [128, NTOK], I32, kind="ExternalInput").ap()
    if "P" in phases:
        xprev = nc.dram_tensor("xprev", [16, 128, NPRE], F32, kind="ExternalInput").ap()
        posprev = nc.dram_tensor("posprev", [128, NPRE], I32, kind="ExternalInput").ap()
    wsrc = nc.dram_tensor("wsrc", [128, WTOT], F32, kind="ExternalInput").ap()
    vecs_d = nc.dram_tensor("vecs", [128, NV], F32, kind="ExternalInput").ap()
    tabs_d = nc.dram_tensor("tabs", [128, HEADS * 128], F32, kind="ExternalInput").ap()
    wbf = nc.dram_tensor("wbf", [128, WTOT], BF16, kind="Internal").ap()
    if "A" in phases and not fused:
        Sd = nc.dram_tensor("s_loc", [HEADS, 128, 2 * DV], F32, kind="ExternalOutput").ap()
    else:
        Sd = nc.dram_tensor("Sd", [HEADS, 128, 2 * DV], F32, kind="Internal").ap()
    if "B" in phases and "P" not in phases:
        s_all = nc.dram_tensor("s_all", [8 * HEADS * 128, 2 * DV], F32, kind="ExternalInput").ap()
    if fused:
        s_all = nc.dram_tensor("Sall", [8 * HEADS * 128, 2 * DV], F32).ap()
    if "B" in phases:
        outT = nc.dram_tensor("outT", [16, 128, nmain * TT], F32, kind="ExternalOutput").ap()

    if debug:
        dbg = nc.dram_tensor("dbg", [8, 16, 128, TT], F32, kind="ExternalOutput").ap()
    S = Sched(nc)
    with ExitStack() as es:
        def sbuf(name, shape, dt):
            return es.enter_context(nc.sbuf_tensor(name, shape, dt))

        def dsem(name):
            return DSem(es.enter_context(nc.semaphore(name)))

        h_t = sbuf("h", [128, 16, TT], F32)
        h_b = [Buf() for _ in range(16)]
        hn_t = sbuf("hn", [128, 16, TT], BF16)
        hn_b = [Buf() for _ in range(16)]
        RA = Region(nc, es, "RA", NF * 1024)
        RB = Region(nc, es, "RB", 32 * 1024)
        Ssl = sbuf("Ssl", [128, 2, 2, DV], F32)
        Ssl_b = [[Buf(), Buf()], [Buf(), Buf()]]
        Sd_b = [Buf() for _ in range(HEADS)]
        Sbf_t = sbuf("Sbf", [128, 2, 2, DV], BF16)
        Sbf_b = [[Buf(), Buf()], [Buf(), Buf()]]
        wr_t = sbuf("wring", [128, NSLOT, SLOT], BF16)
        wr_b = [Buf() for _ in range(NSLOT)]
        wr_sem = [dsem(f"wr{i}") for i in range(NSLOT)]
        vecs = sbuf("vecs_sb", [128, NV], F32)
        vecs_b = Buf()
        tabs = sbuf("tabs_sb", [128, 1, HEADS, 128], F32)
        tabs_b = Buf()
        ident = sbuf("ident", [128, 128], BF16)
        identf = sbuf("identf", [128, 128], F32)
        ones = sbuf("ones", [128, 128], BF16)
        const_b = Buf()
        posi = sbuf("posi", [128, TT], I32)
        posi_b = Buf()
        cosT = sbuf("cosT", [128, TT], F32)
        sinT = sbuf("sinT", [128, TT], F32)
        trig_b = Buf()
        tmpA = sbuf("tmpA", [128, TT], F32)
        tmpA_b = Buf()
        tmpB = sbuf("tmpB", [128, TT], F32)
        tmpB_b = Buf()
        tmpI = sbuf("tmpI", [128, TT], I32)
        tmpI_b = Buf()
        rstd = sbuf("rstd", [128, TT], F32)
        rstd_b = Buf()
        mean = sbuf("mean", [128, TT], F32)
        mean_b = Buf()
        NSQ = 2
        sq_t = sbuf("sq", [128, NSQ, TT], BF16)
        sq_b = [Buf() for _ in range(NSQ)]
        sg_t = sbuf("sg", [128, 2, TT], F32)
        sg_b = [Buf(), Buf()]
        pbf = sbuf("pbf", [128, 2, 2, TT], BF16)
        pbf_b = Buf()
        small = sbuf("small", [128, 8], F32)
        small_b = [Buf() for _ in range(6)]
        junk = sbuf("junk", [128, DV], BF16)
        junk_b = Buf()
        Pm = sbuf("Pm", [128, 2, 128], BF16)
        Pm_b = [Buf(), Buf()]
        ytok = sbuf("ytok", [128, 2, DV], BF16)
        ytok_b = [Buf(), Buf()]
        uh = sbuf("uh", [128, 16, 30], F32)
        uh_b = [Buf() for _ in range(16)]

        banks = [es.enter_context(nc.psum_tensor(f"pb{i}", [128, 512], F32)) for i in range(8)]
        bank_b = [Buf() for _ in range(8)]
        rr = {"mm": 0, "ob": 0}

        def mm_bank():
            i = rr["mm"] % 4
            rr["mm"] += 1
            return banks[i], bank_b[i]

        def ob_bank():
            i = 6 + rr["ob"] % 2
            rr["ob"] += 1
            return banks[i], bank_b[i]

        ST, STb = banks[4], bank_b[4]
        TRf, TRb = banks[5], bank_b[5]
        TR = TRf[:].bitcast(BF16)
        ST2, ST2b = banks[6], bank_b[6]

        ds_const = dsem("ds_const")
        ds_x = dsem("ds_x")
        ds_p = dsem("ds_p")
        ds_pos = dsem("ds_pos")
        ds_out = dsem("ds_out")
        ds_s = dsem("ds_s")
        ds_cc = dsem("ds_cc")
        Sall_b = Buf()
        ds_Sl = [dsem("ds_Sl0"), dsem("ds_Sl1")]
        ds_Ss = [dsem("ds_Ss0"), dsem("ds_Ss1")]
        ds_cast = [dsem(f"ds_cast{i}") for i in range(len(pieces))]

        def V(name, c=0, n=1):
            o = lay[name] + c
            return vecs[:, o:o + n]

        S.dma("act", vecs[:], vecs_d, [], [vecs_b], ds_const)
        S.dma("act", tabs[:].rearrange("p a h i -> p (a h i)"), tabs_d, [], [tabs_b], ds_const)
        S.op("pool", lambda e: e.memset(identf[:], 0.0), [], [const_b])
        S.op("pool", lambda e: e.affine_select(out=identf[:], in_=identf[:], compare_op=ALU.not_equal, fill=1.0,
                                                base=0, pattern=[[-1, 128]], channel_multiplier=1),
             [const_b], [const_b])
        S.copy("dve", ident[:], identf[:], [const_b], [const_b])
        S.op("dve", lambda e: e.memset(ones[:], 1.0), [], [const_b])
        S.op("pool", lambda e: e.memset(uh[:], 0.0), [], uh_b)

        piece_b = [Buf() for _ in pieces]
        need_keys = None
        if phases == {"A"}:
            need_keys = set()
            for hh in range(HEADS):
                need_keys |= {("k", hh), ("v", hh, 0), ("v", hh, 1)}
        need_pieces = set(range(len(pieces))) if need_keys is None else {bidx[k]["piece"] for k in need_keys}
        for pi, (p0, pn) in enumerate(pieces):
            if pi not in need_pieces:
                continue
            S.dma("pool", wbf[:, p0:p0 + pn], wsrc[:, p0:p0 + pn], [], [piece_b[pi]], ds_cast[pi])

        slot_rr = [0]

        def wload(key):
            b = bidx[key]
            s = slot_rr[0] % NSLOT
            slot_rr[0] += 1
            dst = wr_t[:, s, 0:b["nel"]]
            S.dma("sp", dst, wbf[:, b["off"]: b["off"] + b["nel"]], [piece_b[b["piece"]]], [wr_b[s]], wr_sem[s])
            return dst.rearrange("p (k c) -> p k c", k=b["kc"]), wr_b[s]

        def load_tile_inputs(t0, T, with_p, xsrc=None, psrc=None):
            xsrc = xT if xsrc is None else xsrc
            psrc = posb if psrc is None else psrc
            S.dma("act", h_t[:, :, 0:T], xsrc[:, :, t0:t0 + T].rearrange("c p t -> p c t"), [], h_b, ds_x)
            S.dma("act", posi[:, 0:T], psrc[:, t0:t0 + T], [], [posi_b], ds_pos)
            if with_p:
                S.dma("pool", pbf[:, :, :, 0:T], pT[:, :, :, t0:t0 + T].rearrange("l k p t -> p l k t"), [], [pbf_b],
                      ds_p)

        def trig_tables(T):
            S.copy("dve", tmpA[:, 0:T], posi[:, 0:T], [posi_b], [tmpA_b])
            S.ts("dve", tmpA[:, 0:T], tmpA[:, 0:T], V("inv_freq"), None, ALU.mult, None, [tmpA_b, vecs_b], [tmpA_b])
            for (dst, phase) in ((sinT, 0.0), (cosT, 0.25)):
                S.ts("dve", tmpI[:, 0:T], tmpA[:, 0:T], 1.0 / TWO_PI, phase, ALU.mult, ALU.add, [tmpA_b], [tmpI_b])
                S.copy("dve", tmpB[:, 0:T], tmpI[:, 0:T], [tmpI_b], [tmpB_b])
                S.stt("dve", rstd[:, 0:T], tmpB[:, 0:T], -C1, tmpA[:, 0:T], ALU.mult, ALU.add,
                      [tmpB_b, tmpA_b], [rstd_b])
                S.stt("dve", rstd[:, 0:T], tmpB[:, 0:T], -C2, rstd[:, 0:T], ALU.mult, ALU.add,
                      [tmpB_b, rstd_b], [rstd_b])
                if phase != 0.0:
                    S.ts("dve", rstd[:, 0:T], rstd[:, 0:T], phase * TWO_PI, None, ALU.add, None, [rstd_b], [rstd_b])
                S.ts("dve", rstd[:, 0:T], rstd[:, 0:T], -math.pi, math.pi, ALU.max, ALU.min, [rstd_b], [rstd_b])
                S.act(dst[:, 0:T], rstd[:, 0:T], ACT.Sin, [rstd_b], [trig_b])

        def stats_finish_rstd(T, ps, psb, scale, dst, dst_b):
            S.act(dst[:, 0:T], ps[:, 0:T], ACT.Sqrt, [vecs_b], [psb, dst_b], bias=V("eps"), scale=scale)
            S.op("dve", lambda e: e.reciprocal(out=dst[:, 0:T], in_=dst[:, 0:T]), [dst_b], [dst_b])

        sq_rr = [0]

        def sumsq_accumulate(T, src_ap, src_bufs, c, n, eng="act"):
            i = sq_rr[0] % NSQ
            sq_rr[0] += 1
            if eng == "act":
                S.act(sq_t[:, i, 0:T], src_ap, ACT.Square, src_bufs, [sq_b[i]])
            else:
                S.tt(eng, sq_t[:, i, 0:T], src_ap, src_ap, ALU.mult, src_bufs, [sq_b[i]])
            S.mm(ST[:, 0:T], ones[:], sq_t[:, i, 0:T], c == 0, c == n - 1, [sq_b[i], const_b], [STb])

        def pre_norm(T, gname, L):
            for c in range(16):
                sumsq_accumulate(T, h_t[:, c, 0:T], [h_b[c]], c, 16, "act" if c % 2 == 0 else "pool")
            stats_finish_rstd(T, ST, STb, 1.0 / D, rstd, rstd_b)
            for c in range(16):
                eng = "dve"
                S.stt(eng, hn_t[:, c, 0:T], h_t[:, c, 0:T], V((gname, L), c), rstd[:, 0:T], ALU.mult, ALU.mult,
                      [h_b[c], rstd_b, vecs_b], [hn_b[c]])

        def post_norm_residual(T, gname, L):
            stats_finish_rstd(T, ST, STb, 1.0 / D, rstd, rstd_b)
            for c in range(16):
                eng = "dve" if c % 2 == 0 else "pool"
                bb = RB.bufs(c * 2048, c * 2048 + 4 * T)
                Bc = RB.f32(c * 2048, T)
                S.stt("dve", Bc, Bc, V((gname, L), c), rstd[:, 0:T], ALU.mult, ALU.mult, bb + [rstd_b, vecs_b], bb)
                S.tt("pool", h_t[:, c, 0:T], h_t[:, c, 0:T], Bc, ALU.add, bb + [h_b[c]], [h_b[c]])

        def evac_B_and_stats(T, m, ps, psb, bias=None):
            bb = RB.bufs(m * 2048, m * 2048 + 4 * T)
            Bm = RB.f32(m * 2048, T)
            if bias is None:
                S.act(Bm, ps[:, 0:T], ACT.Copy, [], [psb] + bb)
            else:
                S.act(Bm, ps[:, 0:T], ACT.Identity, [vecs_b], [psb] + bb, bias=bias)
            sumsq_accumulate(T, Bm, bb, m, 16, "pool" if m % 2 == 0 else "dve")

        def proj_fm(T, key, m_local, kc, rhs_fn, rhs_bufs_fn, wcache):
            wap, wb = wcache[key]
            ps, psb = mm_bank()
            for k in range(kc):
                S.mm(ps[:, 0:T], wap[:, k, m_local * 128:(m_local + 1) * 128], rhs_fn(k), k == 0, k == kc - 1,
                     [wb] + rhs_bufs_fn(k), [psb])
            return ps, psb

        def hn_rhs(T):
            return (lambda k: hn_t[:, k, 0:T]), (lambda k: [hn_b[k]])

        def head_views(T, par):
            base = par * 14336
            o = {}
            o["qf"] = (base, 2 * 4 * T)
            o["qd"] = (base + 4096, 2 * 2 * T)
            o["kT"] = (base + 6144, 2 * 2 * T)
            o["ktok"] = (base + 8192, 2 * T * 2)
            o["v"] = (base + 10240, 2 * T * 4)
            return o

        def rotary(T, src_lo, dst_lo, eng_pair=("dve", "pool")):
            x1 = RB.f32(src_lo, T)
            x2 = RB.f32(src_lo + 4 * T, T)
            xb = RB.bufs(src_lo, src_lo + 8 * T)
            e0, e1 = eng_pair
            S.tt(e0, tmpA[:, 0:T], x1, cosT[:, 0:T], ALU.mult, xb + [trig_b], [tmpA_b])
            S.tt(e0, tmpB[:, 0:T], x2, sinT[:, 0:T], ALU.mult, xb + [trig_b], [tmpB_b])
            S.tt(e0, mean[:, 0:T], x1, sinT[:, 0:T], ALU.mult, xb + [trig_b], [mean_b])
            S.tt(e0, x2, x2, cosT[:, 0:T], ALU.mult, xb + [trig_b], xb)
            db = RB.bufs(dst_lo, dst_lo + 4 * T)
            S.tt(e0, RB.bf16(dst_lo, T), tmpA[:, 0:T], tmpB[:, 0:T], ALU.subtract, [tmpA_b, tmpB_b], db)
            S.tt(e0, RB.bf16(dst_lo + 2 * T, T), mean[:, 0:T], x2, ALU.add, [mean_b] + xb, db)

        def retention_heads(T, state_only, heads=None):
            nch = T // 128
            for hh in (range(HEADS) if heads is None else heads):
                par = hh % 2
                hv = head_views(T, par)
                wc = {}
                rhs, rhsb = hn_rhs(T)
                qf_lo = hv["qf"][0]
                qfb = RB.bufs(qf_lo, qf_lo + 8 * T)
                qd_lo = hv["qd"][0]
                qdb = RB.bufs(qd_lo, qd_lo + 4 * T)
                kT_lo = hv["kT"][0]
                kTb = RB.bufs(kT_lo, kT_lo + 4 * T)
                kt_lo = hv["ktok"][0]
                ktb = RB.bufs(kt_lo, kt_lo + 4 * T)
                v_lo = hv["v"][0]
                vb = RB.bufs(v_lo, v_lo + 8 * T)
                S.dma("pool", Ssl[:, par].rearrange("p m v -> p (m v)"), Sd[hh], [Sd_b[hh]], Ssl_b[par], ds_Sl[par])
                if not state_only:
                    wc[("q", hh)] = wload(("q", hh))
                    for m in range(2):
                        ps, psb = proj_fm(T, ("q", hh), m, 16, rhs, rhsb, wc)
                        S.act(RB.f32(qf_lo + 4 * T * m, T), ps[:, 0:T], ACT.Copy, [], [psb] + qfb)
                    rotary(T, qf_lo, qd_lo)
                wc[("k", hh)] = wload(("k", hh))
                for m in range(2):
                    ps, psb = proj_fm(T, ("k", hh), m, 16, rhs, rhsb, wc)
                    S.act(RB.f32(qf_lo + 4 * T * m, T), ps[:, 0:T], ACT.Copy, [], [psb] + qfb, scale=DK ** -0.5)
                rotary(T, qf_lo, kT_lo)
                for c in range(nch):
                    for m in range(2):
                        S.tr(TR[:, (c * 2 + m) * 128:(c * 2 + m + 1) * 128],
                             RB.bf16(kT_lo + 2 * T * m + 256 * c, 128), ident[:], kTb + [const_b], [TRb])
                S.act(RB.bf16(kt_lo, 2 * T), TR[:, 0:2 * T], ACT.Copy, [vecs_b], [TRb] + ktb,
                      scale=V("kdec", hh))
                vps = [mm_bank() for _ in range(nch)]
                for half in range(2):
                    wap, wb = wload(("v", hh, half))
                    for c in range(nch):
                        ps, psb = vps[c]
                        for k in range(8):
                            kk = half * 8 + k
                            S.mm(ps[:, 0:DV], hn_t[:, kk, c * 128:(c + 1) * 128], wap[:, k, :],
                                 kk == 0, kk == 15, [wb, hn_b[kk]], [psb])
                for c in range(nch):
                    ps, psb = vps[c]
                    S.copy("act", RB.bf16(v_lo + 1024 * c, DV), ps[:, 0:DV], [], [psb] + vb)
                if not state_only:
                    gps = [mm_bank() for _ in range(nch)]
                    for half in range(2):
                        wap, wb = wload(("g", hh, half))
                        for c in range(nch):
                            ps, psb = gps[c]
                            for k in range(8):
                                kk = half * 8 + k
                                S.mm(ps[:, 0:DV], hn_t[:, kk, c * 128:(c + 1) * 128], wap[:, k, :],
                                     kk == 0, kk == 15, [wb, hn_b[kk]], [psb])
                    sg_lo = 32768 + par * 4096
                    sgb = RA.bufs(sg_lo, sg_lo + 1024 * nch)
                    for c in range(nch):
                        ps, psb = gps[c]
                        S.act(RA.bf16(sg_lo + 1024 * c, DV), ps[:, 0:DV], ACT.Silu, [], [psb] + sgb)
                for c in range(nch):
                    vc = RB.bf16(v_lo + 1024 * c, DV)
                    ktc = [RB.bf16(kt_lo + 512 * c + 256 * m, 128) for m in range(2)]
                    if not state_only:
                        qdc = [RB.bf16(qd_lo + 2 * T * m + 256 * c, 128) for m in range(2)]
                        kTc = [RB.bf16(kT_lo + 2 * T * m + 256 * c, 128) for m in range(2)]
                        for m in range(2):
                            S.copy("act", Sbf_t[:, par, m, :], Ssl[:, par, m, :], [Ssl_b[par][m]],
                                   [Sbf_b[par][m]])
                        S.mm(ST[:, 0:128], kTc[0], qdc[0], True, False, kTb + qdb, [STb])
                        S.mm(ST[:, 0:128], kTc[1], qdc[1], False, True, kTb + qdb, [STb])
                        pi = c % 2
                        S.tt("dve", Pm[:, pi, :], ST[:, 0:128], tabs[:, 0, hh, :], ALU.mult, [tabs_b],
                             [STb, Pm_b[pi]])
                        ops_, opb = ob_bank()
                        S.mm(ops_[:, 0:DV], Pm[:, pi, :], vc, True, False, [Pm_b[pi]] + vb, [opb])
                        S.mm(ops_[:, 0:DV], qdc[0], Sbf_t[:, par, 0, :], False, False, qdb + [Sbf_b[par][0]], [opb])
                        S.mm(ops_[:, 0:DV], qdc[1], Sbf_t[:, par, 1, :], False, True, qdb + [Sbf_b[par][1]], [opb])
                        si = c % 2
                        S.act(junk[:], ops_[:, 0:DV], ACT.Square, [], [opb, junk_b, small_b[si]],
                              accum_out=small[:, si:si + 1])
                        S.act(small[:, 2 + si:3 + si], small[:, si:si + 1], ACT.Sqrt, [vecs_b, small_b[si]],
                              [small_b[2 + si]], bias=V("eps"), scale=V("g2s", hh))
                        S.op("dve", lambda e, a=small[:, 2 + si:3 + si]: e.reciprocal(out=a, in_=a),
                             [small_b[2 + si]], [small_b[2 + si]])
                        S.tt("dve", small[:, 4 + si:5 + si], small[:, 2 + si:3 + si], V("g1", hh), ALU.mult,
                             [small_b[2 + si], vecs_b], [small_b[4 + si]])
                        S.stt("dve", ytok[:, pi, :], ops_[:, 0:DV], small[:, 4 + si:5 + si],
                              RA.bf16(sg_lo + 1024 * c, DV), ALU.mult, ALU.mult,
                              [small_b[4 + si]] + sgb, [opb, ytok_b[pi]])
                        for vcn in range(4):
                            S.tr(TR[:, vcn * 128:(vcn + 1) * 128], ytok[:, pi, vcn * 128:(vcn + 1) * 128], ident[:],
                                 [ytok_b[pi], const_b], [TRb])
                        for vcn in range(4):
                            f = hh * 4 + vcn
                            lo = f * 1024 + 256 * c
                            S.copy("act" if vcn % 2 == 0 else "dve", RA.bf16(lo, 128),
                                   TR[:, vcn * 128:(vcn + 1) * 128], [], [TRb] + RA.bufs(lo, lo + 256))
                    for m in range(2):
                        if state_only and last_chunk_flag[0] and c == nch - 1:
                            break
                        ps, psb = mm_bank()
                        S.mm(ps[:, 0:DV], ktc[m], vc, True, True, ktb + vb, [psb])
                        S.stt("dve", Ssl[:, par, m, :], Ssl[:, par, m, :], sdec[hh], ps[:, 0:DV],
                              ALU.mult, ALU.add, [], [psb, Ssl_b[par][m]])
                S.dma("pool", Sd[hh], Ssl[:, par].rearrange("p m v -> p (m v)"), Ssl_b[par], [Sd_b[hh]], ds_Ss[par])

        last_chunk_flag = [False]

        def mixer_out_proj(T):
            for m in range(16):
                wc = {("wo", m): wload(("wo", m))}
                ps, psb = proj_fm(T, ("wo", m), 0, 32, lambda k: RA.bf16(k * 1024, T),
                                  lambda k: RA.bufs(k * 1024, k * 1024 + 2 * T), wc)
                evac_B_and_stats(T, m, ps, psb)

        def ffn(T, L):
            pre_norm(T, "g_ffn_pre", L)
            rhs, rhsb = hn_rhs(T)
            for fb in range(NF // 2):
                wc = {("fg", L, fb): wload(("fg", L, fb)), ("fu", L, fb): wload(("fu", L, fb))}
                for j in range(2):
                    f = fb * 2 + j
                    psg, psgb = proj_fm(T, ("fg", L, fb), j, 16, rhs, rhsb, wc)
                    psu, psub = proj_fm(T, ("fu", L, fb), j, 16, rhs, rhsb, wc)
                    si = f % 2
                    S.act(sg_t[:, si, 0:T], psg[:, 0:T], ACT.Silu, [], [psgb, sg_b[si]])
                    S.tt("dve", RA.bf16(f * 1024, T), psu[:, 0:T], sg_t[:, si, 0:T], ALU.mult, [sg_b[si]],
                         [psub] + RA.bufs(f * 1024, f * 1024 + 2 * T))
            for m in range(16):
                ps, psb = mm_bank()
                for half in range(2):
                    wap, wb = wload(("fd", L, m, half))
                    for k in range(22):
                        f = half * 22 + k
                        S.mm(ps[:, 0:T], wap[:, k, :], RA.bf16(f * 1024, T), f == 0, f == NF - 1,
                             [wb] + RA.bufs(f * 1024, f * 1024 + 2 * T), [psb])
                evac_B_and_stats(T, m, ps, psb)
            post_norm_residual(T, "g_ffn_post", L)

        def ple(T, L):
            for c in range(16):
                S.copy("pool" if c % 2 == 0 else "act", hn_t[:, c, 0:T], h_t[:, c, 0:T], [h_b[c]], [hn_b[c]])
            rhs, rhsb = hn_rhs(T)
            for mb in range(8):
                wc = {("pg", L, mb): wload(("pg", L, mb))}
                wap, wb = wload(("pp", L, mb))
                for j in range(2):
                    m = mb * 2 + j
                    psg, psgb = proj_fm(T, ("pg", L, mb), j, 16, rhs, rhsb, wc)
                    psp, pspb = mm_bank()
                    for k in range(2):
                        S.mm(psp[:, 0:T], wap[:, k, j * 128:(j + 1) * 128], pbf[:, L, k, 0:T],
                             k == 0, k == 1, [wb, pbf_b], [pspb])
                    si = m % 2
                    S.act(sg_t[:, si, 0:T], psg[:, 0:T], ACT.Sigmoid, [], [psgb, sg_b[si]])
                    bb = RB.bufs(m * 2048, m * 2048 + 4 * T)
                    Bm = RB.f32(m * 2048, T)
                    S.tt("dve", Bm, psp[:, 0:T], sg_t[:, si, 0:T], ALU.mult, [sg_b[si]], [pspb] + bb)
                    sumsq_accumulate(T, Bm, bb, m, 16, "pool")
            post_norm_residual(T, "g_ple", L)

        UW = 30 + TT

        def conv_glu(T, prevT, is_halo):
            pre_norm(T, "g_mix_pre", 1)
            rhs, rhsb = hn_rhs(T)
            for m in range(16):
                ulo = m * UW * 4
                ub_all = RA.bufs(ulo, ulo + UW * 4)
                S.copy("pool", RA.f32(ulo, 30), uh[:, m, 0:30], [uh_b[m]], ub_all)
                wc = {("pw1", m): wload(("pw1", m))}
                ps1, ps1b = proj_fm(T, ("pw1", m), 0, 16, rhs, rhsb, wc)
                ps2, ps2b = proj_fm(T, ("pw1", m), 1, 16, rhs, rhsb, wc)
                si = m % 2
                S.act(sg_t[:, si, 0:T], ps2[:, 0:T], ACT.Sigmoid, [vecs_b], [ps2b, sg_b[si]],
                      bias=V("b_pw1", 16 + m))
                S.stt("dve", RA.f32(ulo + 120, T), ps1[:, 0:T], V("b_pw1", m), sg_t[:, si, 0:T], ALU.add, ALU.mult,
                      [sg_b[si], vecs_b], [ps1b] + ub_all)
                if is_halo:
                    S.ts("dve", RA.f32(ulo + 120, T), RA.f32(ulo + 120, T), V("halo"), None, ALU.mult, None,
                         ub_all + [vecs_b], ub_all)
                S.copy("pool", uh[:, m, 0:30], RA.f32(ulo + 4 * T, 30), ub_all, [uh_b[m]])

        def conv_rest(T):
            for m in range(16):
                ulo = m * UW * 4
                ub_all = RA.bufs(ulo, ulo + UW * 4)
                bb = RB.bufs(m * 2048, m * 2048 + 4 * T)
                Bm = RB.f32(m * 2048, T)
                eng = "dve"
                S.ts(eng, Bm, RA.f32(ulo, T), V("w_dw", m * CW), V("b_dw", m), ALU.mult, ALU.add,
                     ub_all + [vecs_b], bb)
                for k in range(1, CW):
                    S.stt(eng, Bm, RA.f32(ulo + 4 * k, T), V("w_dw", m * CW + k), Bm, ALU.mult, ALU.add,
                          ub_all + [vecs_b] + bb, bb)
                i = sq_rr[0] % NSQ
                sq_rr[0] += 1
                S.copy("act", sq_t[:, i, 0:T], Bm, bb, [sq_b[i]])
                S.mm(ST2[:, 0:T], ones[:], sq_t[:, i, 0:T], m == 0, m == 15, [sq_b[i], const_b], [ST2b])
                sumsq_accumulate(T, Bm, bb, m, 16, "act")
            S.act(mean[:, 0:T], ST2[:, 0:T], ACT.Copy, [], [ST2b, mean_b], scale=1.0 / D)
            S.tt("dve", tmpA[:, 0:T], mean[:, 0:T], mean[:, 0:T], ALU.mult, [mean_b], [tmpA_b])
            S.stt("dve", tmpA[:, 0:T], ST[:, 0:T], 1.0 / D, tmpA[:, 0:T], ALU.mult, ALU.subtract, [tmpA_b],
                  [STb, tmpA_b])
            S.act(rstd[:, 0:T], tmpA[:, 0:T], ACT.Sqrt, [tmpA_b, vecs_b], [rstd_b], bias=V("eps"))
            S.op("dve", lambda e: e.reciprocal(out=rstd[:, 0:T], in_=rstd[:, 0:T]), [rstd_b], [rstd_b])
            for m in range(16):
                bb = RB.bufs(m * 2048, m * 2048 + 4 * T)
                Bm = RB.f32(m * 2048, T)
                eng = "dve" if m % 2 == 0 else "pool"
                S.tt(eng, Bm, Bm, mean[:, 0:T], ALU.subtract, bb + [mean_b], bb)
                S.tt(eng, Bm, Bm, rstd[:, 0:T], ALU.mult, bb + [rstd_b], bb)
                S.act(hn_t[:, m, 0:T], Bm, ACT.Silu, bb + [vecs_b], [hn_b[m]], bias=V("ln_b", m), scale=V("ln_g", m))
            rhs, rhsb = hn_rhs(T)
            for mb in range(8):
                wc = {("pw2", mb): wload(("pw2", mb))}
                for j in range(2):
                    m = mb * 2 + j
                    ps, psb = proj_fm(T, ("pw2", mb), j, 16, rhs, rhsb, wc)
                    evac_B_and_stats(T, m, ps, psb, bias=V("b_pw2", m))
            post_norm_residual(T, "g_mix_post", 1)

        ds_dbg = dsem("ds_dbg")
        final_ops = []

        def dump(stage, from_B):
            if not debug:
                return
            if from_B:
                src = RB.t[:, 0:16 * TT].rearrange("p (c t) -> p c t", c=16)
                bufs = RB.g
            else:
                src = h_t[:]
                bufs = h_b
            final_ops.append(S.dma("act", dbg[stage].rearrange("c p t -> p c t"), src, bufs, [], ds_dbg))

        tiles = [(0, 128)] + [(128 + i * TT, TT) for i in range(nmain)]

        def zero_state():
            S.op("pool", lambda e: e.memset(Ssl[:, 0], 0.0), [], Ssl_b[0])
            for hh in range(HEADS):
                S.dma("pool", Sd[hh], Ssl[:, 0].rearrange("p m v -> p (m v)"), Ssl_b[0], [Sd_b[hh]], ds_Ss[0])

        if "A" in phases:
            zero_state()
            for ti, (t0, T) in enumerate(tiles):
                last_chunk_flag[0] = (ti == len(tiles) - 1)
                load_tile_inputs(t0, T, False)
                trig_tables(T)
                pre_norm(T, "g_mix_pre", 0)
                retention_heads(T, True)
            last_chunk_flag[0] = False
            if not fused:
                final_ops.extend(Sd_b[hh].last_w for hh in range(HEADS))

        if fused:
            S.op("pool", lambda e: e.collective_compute(
                "AllGather", ALU.bypass, replica_groups=[list(range(8))],
                ins=[Sd.rearrange("h p v -> (h p) v").opt()], outs=[s_all.opt()]), Sd_b, [Sall_b], ds_cc, inc=1)

        if "P" in phases:
            zero_state()
            ntp = NPRE // TT
            lg = [math.log1p(-2.0 ** (-5 - hh)) for hh in range(HEADS)]
            for t in range(ntp):
                dist = (ntp - 1 - t) * TT
                heads = [hh for hh in range(HEADS) if dist * (-lg[hh]) < 20.7]
                if not heads:
                    continue
                load_tile_inputs(t * TT, TT, False, xprev, posprev)
                trig_tables(TT)
                pre_norm(TT, "g_mix_pre", 0)
                retention_heads(TT, True, heads)

        if "B" in phases:
            if "P" not in phases:
                for hh in range(HEADS):
                    par = hh % 2
                    S.op("dve", lambda e, a=Ssl[:, par]: e.memset(a, 0.0), [], Ssl_b[par])
                    for i in range(8):
                        bb = RB.bufs(0, 4096)
                        r0 = (i * HEADS + hh) * 128
                        S.dma("act", RB.f32(0, 2 * DV), s_all[r0:r0 + 128, :], [Sall_b], bb, ds_s)
                        S.stt("dve", Ssl[:, par].rearrange("p m v -> p (m v)"), RB.f32(0, 2 * DV),
                              V("coef", i * 8 + hh), Ssl[:, par].rearrange("p m v -> p (m v)"), ALU.mult, ALU.add,
                              bb + [vecs_b] + Ssl_b[par], Ssl_b[par])
                    S.dma("pool", Sd[hh], Ssl[:, par].rearrange("p m v -> p (m v)"), Ssl_b[par], [Sd_b[hh]],
                          ds_Ss[par])
            prevT = None
            for ti, (t0, T) in enumerate(tiles):
                is_halo = (ti == 0)
                load_tile_inputs(t0, T, True)
                trig_tables(T)
                pre_norm(T, "g_mix_pre", 0)
                retention_heads(T, False)
                mixer_out_proj(T)
                if ti == 1:
                    dump(0, True)
                post_norm_residual(T, "g_mix_post", 0)
                if ti == 1:
                    dump(1, False)
                ffn(T, 0)
                if ti == 1:
                    dump(2, False)
                ple(T, 0)
                if ti == 1:
                    dump(3, False)
                conv_glu(T, prevT, is_halo)
                prevT = T
                if is_halo:
                    continue
                conv_rest(T)
                if ti == 1:
                    dump(5, False)
                ffn(T, 1)
                if ti == 1:
                    dump(6, False)
                ple(T, 1)
                if ti == 1:
                    dump(7, False)
                o0 = t0 - 128
                final_ops.append(S.dma("act", outT[:, :, o0:o0 + T].rearrange("c p t -> p c t"), h_t[:, :, 0:T], h_b, [],
                                       ds_out))

        S.emit(es, "act", final_ops)
    return nc, S


def _core_inputs(inputs, nmain, wsrc, tabs):
    x = np.asarray(inputs["x"], np.float32)
    p = np.asarray(inputs["p"], np.float32)
    pos = np.asarray(inputs["positions"], np.int32)
    B, SEQ, _ = x.shape
    seg = nmain * TT
    nseg = SEQ // seg
    assert nseg == 4 and B == 2
    maps = []
    for c in range(8):
        b, j = c // nseg, c % nseg
        s0 = j * seg - 128
        NTOK = 128 + seg
        xs = np.zeros((NTOK, D), np.float32)
        ps = np.zeros((2, NTOK, PLE), np.float32)
        po = np.zeros((NTOK,), np.int32)
        lo = max(s0, 0)
        xs[lo - s0:] = x[b, lo:s0 + NTOK]
        ps[:, lo - s0:] = p[:, b, lo:s0 + NTOK]
        po[lo - s0:] = pos[b, lo:s0 + NTOK]
        NPRE = 3 * seg
        xp = np.zeros((NPRE, D), np.float32)
        pp = np.zeros((NPRE,), np.int32)
        npre = max(s0, 0)
        if npre > 0:
            xp[NPRE - npre:] = x[b, :npre]
            pp[NPRE - npre:] = pos[b, :npre]
        m = {
            "xprev": np.ascontiguousarray(xp.T.reshape(16, 128, NPRE)),
            "posprev": np.ascontiguousarray(np.broadcast_to(pp[None, :], (128, NPRE))),
            "xT": np.ascontiguousarray(xs.T.reshape(16, 128, NTOK)),
            "pT": np.ascontiguousarray(ps.transpose(0, 2, 1).reshape(2, 2, 128, NTOK)),
            "posb": np.ascontiguousarray(np.broadcast_to(po[None, :], (128, NTOK))),
            "wsrc": wsrc,
            "vecs": pack_vecs(inputs, j, seg, b),
            "tabs": tabs.reshape(128, -1),
        }
        maps.append(m)
    return maps


def run(inputs, nmain):
    inputs = {k: np.asarray(v) for k, v in inputs.items()}
    blks, WTOT = weight_blocks()
    wsrc = pack_weights(inputs, blks, WTOT)
    tabs = pack_tabs()
    maps = _core_inputs(inputs, nmain, wsrc, tabs)
    seg = nmain * TT
    ncB, _ = build_program(nmain, {"P", "B"})
    resB = run_bass_kernel_spmd(ncB, maps, core_ids=list(range(8)))
    B = 2
    out = np.empty((B, 4 * seg, D), np.float32)
    for c in range(8):
        b, j = c // 4, c % 4
        oT = np.asarray(resB.results[c]["outT"]).reshape(D, seg)
        out[b, j * seg:(j + 1) * seg] = oT.T
    return out


def kernel(**inputs):
    return run(inputs, 8)
```

```python
import math
from contextlib import ExitStack

import numpy as np
import concourse.bass as bass
import concourse.mybir as mybir
from concourse.bass_utils import run_bass_kernel_spmd

F32 = mybir.dt.float32
BF16 = mybir.dt.bfloat16
I32 = mybir.dt.int32
ACT = mybir.ActivationFunctionType
ALU = mybir.AluOpType

D = 2048
NC16 = 16
HEADS = 8
DK = 256
DV = 512
VW = 4096
FF = 5632
NF = 44
PLE = 256
CW = 31
EPS = 1e-6
TT = 512
NSLOT = 4
SLOT = 4096
PIECE = 32768
TWO_PI = 2.0 * math.pi
C1 = 6.28125
C2 = TWO_PI - C1


class Buf:
    __slots__ = ("last_w", "readers")

    def __init__(self):
        self.last_w = None
        self.readers = []


class DSem:
    __slots__ = ("handle", "count")

    def __init__(self, handle):
        self.handle = handle
        self.count = 0


class Op:
    __slots__ = ("eng", "fn", "deps", "signal", "tok", "dsem", "is_dma", "seq", "inc")

    def __init__(self, eng, fn, dsem, seq, inc=16):
        self.inc = inc
        self.eng = eng
        self.fn = fn
        self.deps = []
        self.signal = False
        self.tok = None
        self.dsem = dsem
        self.is_dma = dsem is not None
        self.seq = seq


class Sched:
    ENGS = ("pe", "act", "dve", "pool", "sp")
    ENGOBJ = {"pe": "tensor", "act": "scalar", "dve": "vector", "pool": "gpsimd", "sp": "sync"}

    def __init__(self, nc):
        self.nc = nc
        self.ops = {e: [] for e in self.ENGS}
        self.nops = 0
        self.nwaits = 0

    @staticmethod
    def _add_dep(deps, d):
        if not d.is_dma:
            for i, x in enumerate(deps):
                if (not x.is_dma) and x.eng == d.eng:
                    if d.seq > x.seq:
                        deps[i] = d
                    return
        else:
            for x in deps:
                if x is d:
                    return
        deps.append(d)

    def op(self, eng, fn, reads=(), writes=(), dsem=None, inc=16):
        o = Op(eng, fn, dsem, self.nops, inc)
        self.nops += 1
        deps = o.deps
        for b in reads:
            if b.last_w is not None:
                self._add_dep(deps, b.last_w)
        for b in writes:
            if b.last_w is not None:
                self._add_dep(deps, b.last_w)
            for r in b.readers:
                self._add_dep(deps, r)
        for b in reads:
            self._add_dep(b.readers, o)
        for b in writes:
            b.last_w = o
            b.readers = []
        self.ops[eng].append(o)
        return o

    def dma(self, q, out, in_, reads, writes, dsem):
        return self.op(q, lambda e: e.dma_start(out=out, in_=in_), reads, writes, dsem)

    def mm(self, out, lhsT, rhs, start, stop, reads, writes):
        return self.op("pe", lambda e: e.matmul(out, lhsT=lhsT, rhs=rhs, start=start, stop=stop), reads, writes)

    def tr(self, out, in_, ident, reads, writes):
        return self.op("pe", lambda e: e.transpose(out, in_, ident), reads, writes)

    def act(self, out, in_, func, reads, writes, bias=None, scale=None, accum_out=None):
        kw = {}
        if bias is not None:
            kw["bias"] = bias
        if scale is not None:
            kw["scale"] = scale
        if accum_out is not None:
            kw["accum_out"] = accum_out
        return self.op("act", lambda e: e.activation(out=out, in_=in_, func=func, **kw), reads, writes)

    def copy(self, eng, out, in_, reads, writes):
        if eng == "act":
            return self.act(out, in_, ACT.Copy, reads, writes)
        return self.op(eng, lambda e: e.tensor_copy(out=out, in_=in_), reads, writes)

    def tt(self, eng, out, in0, in1, op, reads, writes):
        return self.op(eng, lambda e: e.tensor_tensor(out=out, in0=in0, in1=in1, op=op), reads, writes)

    def ts(self, eng, out, in0, s1, s2, op0, op1, reads, writes):
        if op1 is None:
            return self.op(eng, lambda e: e.tensor_scalar(out=out, in0=in0, scalar1=s1, scalar2=None, op0=op0),
                           reads, writes)
        return self.op(eng, lambda e: e.tensor_scalar(out=out, in0=in0, scalar1=s1, scalar2=s2, op0=op0, op1=op1),
                       reads, writes)

    def stt(self, eng, out, in0, scalar, in1, op0, op1, reads, writes):
        return self.op(eng, lambda e: e.scalar_tensor_tensor(out=out, in0=in0, scalar=scalar, in1=in1,
                                                             op0=op0, op1=op1), reads, writes)

    def emit(self, es, final_eng, final_ops):
        nc = self.nc
        for e in self.ENGS:
            for o in self.ops[e]:
                nd = []
                for d in o.deps:
                    if (not d.is_dma) and (not o.is_dma) and d.eng == "pe" and o.eng == "pe":
                        continue
                    d.signal = True
                    nd.append(d)
                o.deps = nd
        for o in final_ops:
            o.signal = True
        for e in self.ENGS:
            cnt = 0
            for o in self.ops[e]:
                if o.is_dma:
                    o.dsem.count += o.inc
                    o.tok = (o.dsem.handle, o.dsem.count)
                elif o.signal:
                    cnt += 1
                    o.tok = (e, cnt)
        esem = {e: es.enter_context(nc.semaphore("es_" + e)) for e in ("pe", "act", "dve", "pool")}

        def run_engine(e, eng):
            waited = {}

            def wait_for(deps):
                need = {}
                for d in deps:
                    s, v = d.tok
                    if need.get(s, 0) < v:
                        need[s] = v
                for s, v in need.items():
                    if waited.get(s, 0) < v:
                        waited[s] = v
                        eng.wait_ge(esem[s] if isinstance(s, str) else s, v)
                        self.nwaits += 1

            for o in self.ops[e]:
                wait_for(o.deps)
                ins = o.fn(eng)
                if o.is_dma:
                    ins.then_inc(o.dsem.handle, o.inc)
                elif o.signal:
                    ins.then_inc(esem[e], 1)
            if e == final_eng:
                wait_for(final_ops)

        with nc.Block() as block:
            for e in self.ENGS:
                if not self.ops[e] and e != final_eng:
                    continue
                getattr(block, self.ENGOBJ[e])(lambda eng, e=e: run_engine(e, eng))


class Region:
    def __init__(self, nc, es, name, nbytes, gran=1024):
        assert nbytes % 4 == 0
        self.t = es.enter_context(nc.sbuf_tensor(name, [128, nbytes // 4], F32))
        self.nbytes = nbytes
        self.gran = gran
        self.g = [Buf() for _ in range((nbytes + gran - 1) // gran)]

    def bufs(self, lo, hi):
        return self.g[lo // self.gran:(hi - 1) // self.gran + 1]

    def f32(self, lo, n):
        return self.t[:, lo // 4: lo // 4 + n]

    def bf16(self, lo, n):
        e0 = lo // 2
        f0 = e0 // 2
        f1 = (e0 + n + 1) // 2
        return self.t[:, f0:f1].bitcast(BF16)[:, e0 - 2 * f0: e0 - 2 * f0 + n]


def weight_blocks():
    blks = []

    def add(key, src, li, r0, kc, cols):
        n = sum(c[1] for c in cols)
        blks.append(dict(key=key, src=src, li=li, r0=r0, kc=kc, cols=cols, ncols=n, nel=kc * n))

    for hh in reversed(range(HEADS)):
        add(("pk", hh), "ret_w_in", 0, 0, 16, [(D + hh * DK, DK)])
        for half in range(2):
            add(("pv", hh, half), "ret_w_in", 0, half * 1024, 8, [(2 * D + hh * DV, DV)])
    for hh in range(HEADS):
        add(("q", hh), "ret_w_in", 0, 0, 16, [(hh * DK, DK)])
        add(("k", hh), "ret_w_in", 0, 0, 16, [(D + hh * DK, DK)])
        for half in range(2):
            add(("v", hh, half), "ret_w_in", 0, half * 1024, 8, [(2 * D + hh * DV, DV)])
        for half in range(2):
            add(("g", hh, half), "ret_w_in", 0, half * 1024, 8, [(2 * D + VW + hh * DV, DV)])
    for m in range(NC16):
        add(("wo", m), "ret_w_out", 0, 0, 32, [(m * 128, 128)])
    for L in range(2):
        if L == 1:
            for m in range(NC16):
                add(("pw1", m), "conv_w_pw1", 0, 0, 16, [(m * 128, 128), (D + m * 128, 128)])
            for m in range(NC16):
                add(("dw", m), "dwdiag", m, 0, CW, [(0, 128)])
            for mb in range(8):
                add(("pw2", mb), "conv_w_pw2", 0, 0, 16, [(mb * 256, 256)])
        for fb in range(NF // 2):
            add(("fg", L, fb), "ffn_w_gate", L, 0, 16, [(fb * 256, 256)])
            add(("fu", L, fb), "ffn_w_up", L, 0, 16, [(fb * 256, 256)])
        for m in range(NC16):
            for half in range(2):
                add(("fd", L, m, half), "ffn_w_down", L, half * 22 * 128, 22, [(m * 128, 128)])
        for mb in range(8):
            add(("pg", L, mb), "ple_w_gate", L, 0, 16, [(mb * 256, 256)])
            add(("pp", L, mb), "ple_w_proj", L, 0, 2, [(mb * 256, 256)])

    off = 0
    for b in blks:
        b["off"] = off
        off += b["nel"]
    return blks, off


def cast_pieces(blks):
    pieces = []
    cur0, cur = 0, 0
    for b in blks:
        cap = 8192 if len(pieces) < 8 else PIECE
        if cur + b["nel"] > cap and cur > 0:
            pieces.append((cur0, cur))
            cur0 += cur
            cur = 0
        b["piece"] = len(pieces)
        cur += b["nel"]
    pieces.append((cur0, cur))
    return pieces


def pack_weights(inputs, blks, total):
    ws = np.empty((128, total), np.float32)
    for b in blks:
        if b["src"] == "dwdiag":
            m = b["li"]
            wd = inputs["conv_w_dw"][0][:, m * 128:(m + 1) * 128]
            sub = np.zeros((128, CW, 128), np.float32)
            ar = np.arange(128)
            sub[ar, :, ar] = wd.T
            ws[:, b["off"]: b["off"] + b["nel"]] = sub.reshape(128, b["nel"])
            continue
        W = inputs[b["src"]][b["li"]]
        rows = W[b["r0"]: b["r0"] + b["kc"] * 128]
        sub = np.concatenate([rows[:, c0:c0 + n] for (c0, n) in b["cols"]], axis=1)
        sub = sub.reshape(b["kc"], 128, b["ncols"]).transpose(1, 0, 2).reshape(128, b["nel"])
        ws[:, b["off"]: b["off"] + b["nel"]] = sub
    return ws


def vec_layout():
    lay = {}
    off = 0

    def add(name, n):
        nonlocal off
        lay[name] = off
        off += n

    for L in range(2):
        for nm in ("g_mix_pre", "g_mix_post", "g_ffn_pre", "g_ffn_post", "g_ple"):
            add((nm, L), 16)
    add("b_pw1", 32)
    add("b_dw", 16)
    add("ln_g", 16)
    add("ln_b", 16)
    add("b_pw2", 16)
    add("w_dw", 16 * CW)
    add("inv_freq", 1)
    add("halo", 1)
    add("kdec", 8)
    add("g1", 8)
    add("g2s", 8)
    add("coef", 64)
    add("eps", 1)
    return lay, off


def chunked(v):
    return np.asarray(v, np.float32).reshape(-1, 128).T


def pack_vecs(inputs, core_j, seg, core_b=0, nseg=4):
    lay, n = vec_layout()
    V = np.zeros((128, n), np.float32)
    for L in range(2):
        for nm in ("g_mix_pre", "g_mix_post", "g_ffn_pre", "g_ffn_post", "g_ple"):
            V[:, lay[(nm, L)]: lay[(nm, L)] + 16] = chunked(inputs[nm][L])
    V[:, lay["b_pw1"]: lay["b_pw1"] + 32] = chunked(inputs["conv_b_pw1"][0])
    V[:, lay["b_dw"]: lay["b_dw"] + 16] = chunked(inputs["conv_b_dw"][0])
    V[:, lay["ln_g"]: lay["ln_g"] + 16] = chunked(inputs["conv_ln_g"][0])
    V[:, lay["ln_b"]: lay["ln_b"] + 16] = chunked(inputs["conv_ln_b"][0])
    V[:, lay["b_pw2"]: lay["b_pw2"] + 16] = chunked(inputs["conv_b_pw2"][0])
    wd = inputs["conv_w_dw"][0]
    for m in range(16):
        V[:, lay["w_dw"] + m * CW: lay["w_dw"] + (m + 1) * CW] = wd[:, m * 128:(m + 1) * 128].T
    half = 128
    inv_freq = (np.float32(10000.0) ** (-np.arange(half, dtype=np.float32) / np.float32(half))).astype(np.float32)
    V[:, lay["inv_freq"]] = inv_freq
    V[:, lay["halo"]] = 0.0 if core_j == 0 else 1.0
    lg = np.log1p(-np.exp2(-5.0 - np.arange(HEADS, dtype=np.float32))).astype(np.float32)
    idx = np.arange(128, dtype=np.float32)
    V[:, lay["kdec"]: lay["kdec"] + 8] = np.exp(lg[None, :] * (127.0 - idx)[:, None])
    lg64 = lg.astype(np.float64)
    V[:, lay["g1"]: lay["g1"] + 8] = np.exp(lg64[None, :] * (idx.astype(np.float64) + 1.0)[:, None])
    V[:, lay["g2s"]: lay["g2s"] + 8] = np.exp(2.0 * lg64[None, :] * (idx.astype(np.float64) + 1.0)[:, None]) / DV
    for r in range(8):
        rb, i = r // nseg, r % nseg
        for hh in range(HEADS):
            c = 0.0
            if rb == core_b and i < core_j:
                c = math.exp(float(np.float64(lg[hh])) * seg * (core_j - 1 - i))
            V[:, lay["coef"] + r * 8 + hh] = c
    V[:, lay["eps"]] = EPS
    return V


def pack_tabs():
    lg = np.log1p(-np.exp2(-5.0 - np.arange(HEADS, dtype=np.float32))).astype(np.float64)
    idx = np.arange(128, dtype=np.float64)
    T = np.zeros((128, 1, HEADS, 128), np.float32)
    for hh in range(HEADS):
        causal = (idx[None, :] >= idx[:, None]).astype(np.float64)
        T[:, 0, hh, :] = np.exp(-lg[hh] * (idx[:, None] + 1.0)) * causal
    return T


def state_decay():
    lg = np.log1p(-np.exp2(-5.0 - np.arange(HEADS, dtype=np.float32))).astype(np.float64)
    return [float(np.exp(lg[h] * 128.0)) for h in range(HEADS)]


def build_program(nmain, phases, debug=False):
    NTOK = 128 + nmain * TT
    blks, WTOT = weight_blocks()
    pieces = cast_pieces(blks)
    bidx = {b["key"]: b for b in blks}
    lay, NV = vec_layout()
    sdec = state_decay()
    fused = False
    NPRE = 3 * nmain * TT

    nc = bass.Bass("TRN2", target_bir_lowering=False)
    xT = nc.dram_tensor("xT", [16, 128, NTOK], F32, kind="ExternalInput").ap()
    pT = nc.dram_tensor("pT", [2, 2, 128, NTOK], F32, kind="ExternalInput").ap()
    posb = nc.dram_tensor("posb", [128, NTOK], I32, kind="ExternalInput").ap()
    if "P" in phases:
        xprev = nc.dram_tensor("xprev", [16, 128, NPRE], F32, kind="ExternalInput").ap()
        posprev = nc.dram_tensor("posprev", [128, NPRE], I32, kind="ExternalInput").ap()
    wsrc = nc.dram_tensor("wsrc", [128, WTOT], F32, kind="ExternalInput").ap()
    vecs_d = nc.dram_tensor("vecs", [128, NV], F32, kind="ExternalInput").ap()
    tabs_d = nc.dram_tensor("tabs", [128, HEADS * 128], F32, kind="ExternalInput").ap()
    WSPLIT = [p0 for (p0, pn) in pieces if p0 >= WTOT // 2][0]
    wbfA = nc.dram_tensor("wbfA", [128, WSPLIT], BF16, kind="Internal").ap()
    wbfB = nc.dram_tensor("wbfB", [128, WTOT - WSPLIT], BF16, kind="Internal").ap()

    def wbf_slice(o0, n):
        if o0 >= WSPLIT:
            return wbfB[:, o0 - WSPLIT:o0 - WSPLIT + n]
        assert o0 + n <= WSPLIT
        return wbfA[:, o0:o0 + n]

    if "A" in phases and not fused:
        Sd = nc.dram_tensor("s_loc", [HEADS, 128, 2 * DV], F32, kind="ExternalOutput").ap()
    else:
        Sd = nc.dram_tensor("Sd", [HEADS, 128, 2 * DV], F32, kind="Internal").ap()
    if "B" in phases and "P" not in phases:
        s_all = nc.dram_tensor("s_all", [8 * HEADS * 128, 2 * DV], F32, kind="ExternalInput").ap()
    if fused:
        s_all = nc.dram_tensor("Sall", [8 * HEADS * 128, 2 * DV], F32).ap()
    if "B" in phases:
        outT = nc.dram_tensor("outT", [16, 128, nmain * TT], F32, kind="ExternalOutput").ap()

    if debug:
        dbg = nc.dram_tensor("dbg", [8, 16, 128, TT], F32, kind="ExternalOutput").ap()
    S = Sched(nc)
    with ExitStack() as es:
        def sbuf(name, shape, dt):
            return es.enter_context(nc.sbuf_tensor(name, shape, dt))

        def dsem(name):
            return DSem(es.enter_context(nc.semaphore(name)))

        h_t = sbuf("h", [128, 16, TT], F32)
        h_b = [Buf() for _ in range(16)]
        hn_t = sbuf("hn", [128, 16, TT], BF16)
        hn_b = [Buf() for _ in range(16)]
        RA = Region(nc, es, "RA", NF * 1024)
        RB = Region(nc, es, "RB", 32 * 1024)
        Ssl = sbuf("Ssl", [128, 2, 2, DV], F32)
        Ssl_b = [[Buf(), Buf()], [Buf(), Buf()]]
        Sd_b = [Buf() for _ in range(HEADS)]
        Sbf_t = sbuf("Sbf", [128, 2, 2, DV], BF16)
        Sbf_b = [[Buf(), Buf()], [Buf(), Buf()]]
        wr_t = sbuf("wring", [128, NSLOT, SLOT], BF16)
        wr_b = [Buf() for _ in range(NSLOT)]
        wr_sem = [dsem(f"wr{i}") for i in range(NSLOT)]
        vecs = sbuf("vecs_sb", [128, NV], F32)
        vecs_b = Buf()
        tabs = sbuf("tabs_sb", [128, 1, HEADS, 128], F32)
        tabs_b = Buf()
        ident = sbuf("ident", [128, 128], BF16)
        identf = sbuf("identf", [128, 128], F32)
        ones = sbuf("ones", [128, 128], BF16)
        const_b = Buf()
        posi = sbuf("posi", [128, TT], I32)
        posi_b = Buf()
        cosT = sbuf("cosT", [128, TT], F32)
        sinT = sbuf("sinT", [128, TT], F32)
        trig_b = Buf()
        tmpA = sbuf("tmpA", [128, TT], F32)
        tmpA_b = Buf()
        tmpB = sbuf("tmpB", [128, TT], F32)
        tmpB_b = Buf()
        tmpI = sbuf("tmpI", [128, TT], I32)
        tmpI_b = Buf()
        rstd = sbuf("rstd", [128, TT], F32)
        rstd_b = Buf()
        mean = sbuf("mean", [128, TT], F32)
        mean_b = Buf()
        NSQ = 2
        sq_t = sbuf("sq", [128, NSQ, TT], BF16)
        sq_b = [Buf() for _ in range(NSQ)]
        sg_t = sbuf("sg", [128, 2, TT], F32)
        sg_b = [Buf(), Buf()]
        pbf = sbuf("pbf", [128, 2, 2, TT], BF16)
        pbf_b = Buf()
        small = sbuf("small", [128, 8], F32)
        small_b = [Buf() for _ in range(6)]
        junk = sbuf("junk", [128, DV], BF16)
        junk_b = Buf()
        Pm = sbuf("Pm", [128, 2, 128], BF16)
        Pm_b = [Buf(), Buf()]
        ytok = sbuf("ytok", [128, 2, DV], BF16)
        ytok_b = [Buf(), Buf()]
        uh = sbuf("uh", [128, 16, 30], BF16)
        uh_b = [Buf() for _ in range(16)]

        banks = [es.enter_context(nc.psum_tensor(f"pb{i}", [128, 512], F32)) for i in range(8)]
        bank_b = [Buf() for _ in range(8)]
        rr = {"mm": 0, "ob": 0}

        def mm_bank():
            i = rr["mm"] % 4
            rr["mm"] += 1
            return banks[i], bank_b[i]

        def ob_bank():
            i = 6 + rr["ob"] % 2
            rr["ob"] += 1
            return banks[i], bank_b[i]

        ST, STb = banks[4], bank_b[4]
        TRf, TRb = banks[5], bank_b[5]
        TR = TRf[:].bitcast(BF16)
        ST2, ST2b = banks[6], bank_b[6]

        ds_const = dsem("ds_const")
        ds_x = dsem("ds_x")
        ds_p = dsem("ds_p")
        ds_pos = dsem("ds_pos")
        ds_out = dsem("ds_out")
        ds_s = dsem("ds_s")
        ds_cc = dsem("ds_cc")
        Sall_b = Buf()
        ds_Sl = [dsem("ds_Sl0"), dsem("ds_Sl1")]
        ds_Ss = [dsem("ds_Ss0"), dsem("ds_Ss1")]
        ds_cast = [dsem(f"ds_cast{i}") for i in range(len(pieces))]

        def V(name, c=0, n=1):
            o = lay[name] + c
            return vecs[:, o:o + n]

        S.dma("act", vecs[:], vecs_d, [], [vecs_b], ds_const)
        S.dma("act", tabs[:].rearrange("p a h i -> p (a h i)"), tabs_d, [], [tabs_b], ds_const)
        S.op("pool", lambda e: e.memset(identf[:], 0.0), [], [const_b])
        S.op("pool", lambda e: e.affine_select(out=identf[:], in_=identf[:], compare_op=ALU.not_equal, fill=1.0,
                                                base=0, pattern=[[-1, 128]], channel_multiplier=1),
             [const_b], [const_b])
        S.copy("dve", ident[:], identf[:], [const_b], [const_b])
        S.op("dve", lambda e: e.memset(ones[:], 1.0), [], [const_b])
        S.op("pool", lambda e: e.memset(uh[:], 0.0), [], uh_b)

        piece_b = [Buf() for _ in pieces]
        need_keys = None
        if phases == {"A"}:
            need_keys = set()
            for hh in range(HEADS):
                need_keys |= {("k", hh), ("v", hh, 0), ("v", hh, 1)}
        need_pieces = set(range(len(pieces))) if need_keys is None else {bidx[k]["piece"] for k in need_keys}
        for pi, (p0, pn) in enumerate(pieces):
            if pi not in need_pieces:
                continue
            S.dma("pool", wbf_slice(p0, pn), wsrc[:, p0:p0 + pn], [], [piece_b[pi]], ds_cast[pi])

        slot_rr = [0]

        def wload(key):
            b = bidx[key]
            s = slot_rr[0] % NSLOT
            slot_rr[0] += 1
            dst = wr_t[:, s, 0:b["nel"]]
            S.dma("sp", dst, wbf_slice(b["off"], b["nel"]), [piece_b[b["piece"]]], [wr_b[s]], wr_sem[s])
            return dst.rearrange("p (k c) -> p k c", k=b["kc"]), wr_b[s]

        def load_tile_inputs(t0, T, with_p, xsrc=None, psrc=None):
            xsrc = xT if xsrc is None else xsrc
            psrc = posb if psrc is None else psrc
            S.dma("act", h_t[:, :, 0:T], xsrc[:, :, t0:t0 + T].rearrange("c p t -> p c t"), [], h_b, ds_x)
            S.dma("act", posi[:, 0:T], psrc[:, t0:t0 + T], [], [posi_b], ds_pos)
            if with_p:
                S.dma("pool", pbf[:, :, :, 0:T], pT[:, :, :, t0:t0 + T].rearrange("l k p t -> p l k t"), [], [pbf_b],
                      ds_p)

        def trig_tables(T):
            S.copy("dve", tmpA[:, 0:T], posi[:, 0:T], [posi_b], [tmpA_b])
            S.ts("dve", tmpA[:, 0:T], tmpA[:, 0:T], V("inv_freq"), None, ALU.mult, None, [tmpA_b, vecs_b], [tmpA_b])
            for (dst, phase) in ((sinT, 0.0), (cosT, 0.25)):
                S.ts("dve", tmpI[:, 0:T], tmpA[:, 0:T], 1.0 / TWO_PI, phase, ALU.mult, ALU.add, [tmpA_b], [tmpI_b])
                S.copy("dve", tmpB[:, 0:T], tmpI[:, 0:T], [tmpI_b], [tmpB_b])
                S.stt("dve", rstd[:, 0:T], tmpB[:, 0:T], -C1, tmpA[:, 0:T], ALU.mult, ALU.add,
                      [tmpB_b, tmpA_b], [rstd_b])
                S.stt("dve", rstd[:, 0:T], tmpB[:, 0:T], -C2, rstd[:, 0:T], ALU.mult, ALU.add,
                      [tmpB_b, rstd_b], [rstd_b])
                if phase != 0.0:
                    S.ts("dve", rstd[:, 0:T], rstd[:, 0:T], phase * TWO_PI, None, ALU.add, None, [rstd_b], [rstd_b])
                S.ts("dve", rstd[:, 0:T], rstd[:, 0:T], -math.pi, math.pi, ALU.max, ALU.min, [rstd_b], [rstd_b])
                S.act(dst[:, 0:T], rstd[:, 0:T], ACT.Sin, [rstd_b], [trig_b])

        def stats_finish_rstd(T, ps, psb, scale, dst, dst_b):
            S.act(dst[:, 0:T], ps[:, 0:T], ACT.Sqrt, [vecs_b], [psb, dst_b], bias=V("eps"), scale=scale)
            S.op("dve", lambda e: e.reciprocal(out=dst[:, 0:T], in_=dst[:, 0:T]), [dst_b], [dst_b])

        sq_rr = [0]

        def sumsq_accumulate(T, src_ap, src_bufs, c, n, eng="act"):
            i = sq_rr[0] % NSQ
            sq_rr[0] += 1
            if eng == "act":
                S.act(sq_t[:, i, 0:T], src_ap, ACT.Square, src_bufs, [sq_b[i]])
            else:
                S.tt(eng, sq_t[:, i, 0:T], src_ap, src_ap, ALU.mult, src_bufs, [sq_b[i]])
            S.mm(ST[:, 0:T], ones[:], sq_t[:, i, 0:T], c == 0, c == n - 1, [sq_b[i], const_b], [STb])

        def pre_norm(T, gname, L):
            for c in range(16):
                sumsq_accumulate(T, h_t[:, c, 0:T], [h_b[c]], c, 16, "act" if c % 2 == 0 else "pool")
            stats_finish_rstd(T, ST, STb, 1.0 / D, rstd, rstd_b)
            for c in range(16):
                eng = "dve"
                S.stt(eng, hn_t[:, c, 0:T], h_t[:, c, 0:T], V((gname, L), c), rstd[:, 0:T], ALU.mult, ALU.mult,
                      [h_b[c], rstd_b, vecs_b], [hn_b[c]])

        def post_norm_residual(T, gname, L, out_to_B=False):
            stats_finish_rstd(T, ST, STb, 1.0 / D, rstd, rstd_b)
            for c in range(16):
                eng = "dve" if c % 2 == 0 else "pool"
                bb = RB.bufs(c * 2048, c * 2048 + 4 * T)
                Bc = RB.f32(c * 2048, T)
                S.stt("dve", Bc, Bc, V((gname, L), c), rstd[:, 0:T], ALU.mult, ALU.mult, bb + [rstd_b, vecs_b], bb)
                if out_to_B:
                    S.tt("pool", Bc, h_t[:, c, 0:T], Bc, ALU.add, bb + [h_b[c]], bb)
                else:
                    S.tt("pool", h_t[:, c, 0:T], h_t[:, c, 0:T], Bc, ALU.add, bb + [h_b[c]], [h_b[c]])

        def evac_B_and_stats(T, m, ps, psb, bias=None):
            bb = RB.bufs(m * 2048, m * 2048 + 4 * T)
            Bm = RB.f32(m * 2048, T)
            if bias is None:
                S.act(Bm, ps[:, 0:T], ACT.Copy, [], [psb] + bb)
            else:
                S.act(Bm, ps[:, 0:T], ACT.Identity, [vecs_b], [psb] + bb, bias=bias)
            sumsq_accumulate(T, Bm, bb, m, 16, "pool" if m % 2 == 0 else "dve")

        def proj_fm(T, key, m_local, kc, rhs_fn, rhs_bufs_fn, wcache):
            wap, wb = wcache[key]
            ps, psb = mm_bank()
            for k in range(kc):
                S.mm(ps[:, 0:T], wap[:, k, m_local * 128:(m_local + 1) * 128], rhs_fn(k), k == 0, k == kc - 1,
                     [wb] + rhs_bufs_fn(k), [psb])
            return ps, psb

        def hn_rhs(T):
            return (lambda k: hn_t[:, k, 0:T]), (lambda k: [hn_b[k]])

        def head_views(T, par):
            base = par * 14336
            o = {}
            o["qf"] = (base, 2 * 4 * T)
            o["qd"] = (base + 4096, 2 * 2 * T)
            o["kT"] = (base + 6144, 2 * 2 * T)
            o["ktok"] = (base + 8192, 2 * T * 2)
            o["v"] = (base + 10240, 2 * T * 4)
            return o

        def rotary(T, src_lo, dst_lo, eng_pair=("dve", "pool")):
            x1 = RB.f32(src_lo, T)
            x2 = RB.f32(src_lo + 4 * T, T)
            xb = RB.bufs(src_lo, src_lo + 8 * T)
            e0, e1 = eng_pair
            S.tt(e0, tmpA[:, 0:T], x1, cosT[:, 0:T], ALU.mult, xb + [trig_b], [tmpA_b])
            S.tt(e0, tmpB[:, 0:T], x2, sinT[:, 0:T], ALU.mult, xb + [trig_b], [tmpB_b])
            S.tt(e0, mean[:, 0:T], x1, sinT[:, 0:T], ALU.mult, xb + [trig_b], [mean_b])
            S.tt(e0, x2, x2, cosT[:, 0:T], ALU.mult, xb + [trig_b], xb)
            db = RB.bufs(dst_lo, dst_lo + 4 * T)
            S.tt(e0, RB.bf16(dst_lo, T), tmpA[:, 0:T], tmpB[:, 0:T], ALU.subtract, [tmpA_b, tmpB_b], db)
            S.tt(e0, RB.bf16(dst_lo + 2 * T, T), mean[:, 0:T], x2, ALU.add, [mean_b] + xb, db)

        def head_bufs(T, hh):
            par = hh % 2
            hv = head_views(T, par)
            o = dict(par=par)
            o["qf_lo"] = hv["qf"][0]
            o["qfb"] = RB.bufs(o["qf_lo"], o["qf_lo"] + 8 * T)
            o["qd_lo"] = hv["qd"][0]
            o["qdb"] = RB.bufs(o["qd_lo"], o["qd_lo"] + 4 * T)
            o["kT_lo"] = hv["kT"][0]
            o["kTb"] = RB.bufs(o["kT_lo"], o["kT_lo"] + 4 * T)
            o["kt_lo"] = hv["ktok"][0]
            o["ktb"] = RB.bufs(o["kt_lo"], o["kt_lo"] + 4 * T)
            o["v_lo"] = hv["v"][0]
            o["vb"] = RB.bufs(o["v_lo"], o["v_lo"] + 8 * T)
            o["sg_lo"] = 32768 + par * 4096
            o["sgb"] = RA.bufs(o["sg_lo"], o["sg_lo"] + 1024 * (T // 128))
            return o

        kv_prefix = [False]

        def head_proj(T, hh, state_only):
            nch = T // 128
            hb = head_bufs(T, hh)
            par = hb["par"]
            rhs, rhsb = hn_rhs(T)
            qf_lo, qfb, qd_lo, kT_lo, kTb = hb["qf_lo"], hb["qfb"], hb["qd_lo"], hb["kT_lo"], hb["kTb"]
            kt_lo, ktb, v_lo, vb = hb["kt_lo"], hb["ktb"], hb["v_lo"], hb["vb"]
            S.dma("pool", Ssl[:, par].rearrange("p m v -> p (m v)"), Sd[hh], [Sd_b[hh]], Ssl_b[par], ds_Sl[par])
            wc = {}
            if not state_only:
                wc[("q", hh)] = wload(("q", hh))
                for m in range(2):
                    ps, psb = proj_fm(T, ("q", hh), m, 16, rhs, rhsb, wc)
                    S.act(RB.f32(qf_lo + 4 * T * m, T), ps[:, 0:T], ACT.Copy, [], [psb] + qfb)
                    yield
                rotary(T, qf_lo, qd_lo)
            kkey = ("pk", hh) if kv_prefix[0] else ("k", hh)
            wc[kkey] = wload(kkey)
            for m in range(2):
                ps, psb = proj_fm(T, kkey, m, 16, rhs, rhsb, wc)
                S.act(RB.f32(qf_lo + 4 * T * m, T), ps[:, 0:T], ACT.Copy, [], [psb] + qfb, scale=DK ** -0.5)
                yield
            rotary(T, qf_lo, kT_lo)
            vps = [mm_bank() for _ in range(nch)]
            for half in range(2):
                wap, wb = wload(("pv", hh, half) if kv_prefix[0] else ("v", hh, half))
                for c in range(nch):
                    ps, psb = vps[c]
                    for k in range(8):
                        kk = half * 8 + k
                        S.mm(ps[:, 0:DV], hn_t[:, kk, c * 128:(c + 1) * 128], wap[:, k, :],
                             kk == 0, kk == 15, [wb, hn_b[kk]], [psb])
                    yield
            for c in range(nch):
                ps, psb = vps[c]
                S.copy("act", RB.bf16(v_lo + 1024 * c, DV), ps[:, 0:DV], [], [psb] + vb)
            if not state_only:
                gps = [mm_bank() for _ in range(nch)]
                for half in range(2):
                    wap, wb = wload(("g", hh, half))
                    for c in range(nch):
                        ps, psb = gps[c]
                        for k in range(8):
                            kk = half * 8 + k
                            S.mm(ps[:, 0:DV], hn_t[:, kk, c * 128:(c + 1) * 128], wap[:, k, :],
                                 kk == 0, kk == 15, [wb, hn_b[kk]], [psb])
                        yield
                for c in range(nch):
                    ps, psb = gps[c]
                    S.act(RA.bf16(hb["sg_lo"] + 1024 * c, DV), ps[:, 0:DV], ACT.Silu, [], [psb] + hb["sgb"])
            for c in range(nch):
                for m in range(2):
                    S.tr(TR[:, (c * 2 + m) * 128:(c * 2 + m + 1) * 128],
                         RB.bf16(kT_lo + 2 * T * m + 256 * c, 128), ident[:], kTb + [const_b], [TRb])
            S.act(RB.bf16(kt_lo, 2 * T), TR[:, 0:2 * T], ACT.Copy, [vecs_b], [TRb] + ktb, scale=V("kdec", hh))
            yield

        def head_chunks(T, hh, state_only, skip_last):
            nch = T // 128
            hb = head_bufs(T, hh)
            par = hb["par"]
            qd_lo, qdb, kT_lo, kTb = hb["qd_lo"], hb["qdb"], hb["kT_lo"], hb["kTb"]
            kt_lo, ktb, v_lo, vb, sg_lo, sgb = hb["kt_lo"], hb["ktb"], hb["v_lo"], hb["vb"], hb["sg_lo"], hb["sgb"]
            for c in range(nch):
                vc = RB.bf16(v_lo + 1024 * c, DV)
                ktc = [RB.bf16(kt_lo + 512 * c + 256 * m, 128) for m in range(2)]
                if not state_only:
                    qdc = [RB.bf16(qd_lo + 2 * T * m + 256 * c, 128) for m in range(2)]
                    kTc = [RB.bf16(kT_lo + 2 * T * m + 256 * c, 128) for m in range(2)]
                    for m in range(2):
                        S.copy("act", Sbf_t[:, par, m, :], Ssl[:, par, m, :], [Ssl_b[par][m]], [Sbf_b[par][m]])
                    S.mm(ST[:, 0:128], kTc[0], qdc[0], True, False, kTb + qdb, [STb])
                    S.mm(ST[:, 0:128], kTc[1], qdc[1], False, True, kTb + qdb, [STb])
                    S.tt("dve", Pm[:, par, :], ST[:, 0:128], tabs[:, 0, hh, :], ALU.mult, [tabs_b],
                         [STb, Pm_b[par]])
                    yield
                    ops_, opb = ob_bank()
                    S.mm(ops_[:, 0:DV], Pm[:, par, :], vc, True, False, [Pm_b[par]] + vb, [opb])
                    S.mm(ops_[:, 0:DV], qdc[0], Sbf_t[:, par, 0, :], False, False, qdb + [Sbf_b[par][0]], [opb])
                    S.mm(ops_[:, 0:DV], qdc[1], Sbf_t[:, par, 1, :], False, True, qdb + [Sbf_b[par][1]], [opb])
                    si = par
                    S.act(junk[:], ops_[:, 0:DV], ACT.Square, [], [opb, junk_b, small_b[si]],
                          accum_out=small[:, si:si + 1])
                    S.act(small[:, 2 + si:3 + si], small[:, si:si + 1], ACT.Sqrt, [vecs_b, small_b[si]],
                          [small_b[2 + si]], bias=V("eps"), scale=V("g2s", hh))
                    S.op("dve", lambda e, a=small[:, 2 + si:3 + si]: e.reciprocal(out=a, in_=a),
                         [small_b[2 + si]], [small_b[2 + si]])
                    S.tt("dve", small[:, 4 + si:5 + si], small[:, 2 + si:3 + si], V("g1", hh), ALU.mult,
                         [small_b[2 + si], vecs_b], [small_b[4 + si]])
                    S.stt("dve", ytok[:, par, :], ops_[:, 0:DV], small[:, 4 + si:5 + si],
                          RA.bf16(sg_lo + 1024 * c, DV), ALU.mult, ALU.mult,
                          [small_b[4 + si]] + sgb, [opb, ytok_b[par]])
                    yield
                    for vcn in range(4):
                        S.tr(TR[:, vcn * 128:(vcn + 1) * 128], ytok[:, par, vcn * 128:(vcn + 1) * 128], ident[:],
                             [ytok_b[par], const_b], [TRb])
                    for vcn in range(4):
                        f = hh * 4 + vcn
                        lo = f * 1024 + 256 * c
                        S.copy("act" if vcn % 2 == 0 else "dve", RA.bf16(lo, 128),
                               TR[:, vcn * 128:(vcn + 1) * 128], [], [TRb] + RA.bufs(lo, lo + 256))
                    yield
                if not (skip_last and c == nch - 1):
                    for m in range(2):
                        ps, psb = ob_bank()
                        S.mm(ps[:, 0:DV], ktc[m], vc, True, True, ktb + vb, [psb])
                        S.stt("dve", Ssl[:, par, m, :], Ssl[:, par, m, :], sdec[hh], ps[:, 0:DV],
                              ALU.mult, ALU.add, [], [psb, Ssl_b[par][m]])
                yield
            S.dma("pool", Sd[hh], Ssl[:, par].rearrange("p m v -> p (m v)"), Ssl_b[par], [Sd_b[hh]], ds_Ss[par])

        def retention_heads(T, state_only, heads=None):
            hl = list(range(HEADS) if heads is None else heads)
            skip_last = state_only and last_chunk_flag[0]
            prev = None
            for hh in hl:
                pg = head_proj(T, hh, state_only)
                if prev is None:
                    for _ in pg:
                        pass
                else:
                    a_done = b_done = False
                    while not (a_done and b_done):
                        if not a_done:
                            try:
                                next(pg)
                            except StopIteration:
                                a_done = True
                        if not b_done:
                            try:
                                next(prev)
                            except StopIteration:
                                b_done = True
                prev = head_chunks(T, hh, state_only, skip_last)
            for _ in prev:
                pass

        last_chunk_flag = [False]

        def mixer_out_proj(T):
            for m in range(16):
                wc = {("wo", m): wload(("wo", m))}
                ps, psb = proj_fm(T, ("wo", m), 0, 32, lambda k: RA.bf16(k * 1024, T),
                                  lambda k: RA.bufs(k * 1024, k * 1024 + 2 * T), wc)
                evac_B_and_stats(T, m, ps, psb)

        def ffn(T, L):
            pre_norm(T, "g_ffn_pre", L)
            rhs, rhsb = hn_rhs(T)
            for fb in range(NF // 2):
                wc = {("fg", L, fb): wload(("fg", L, fb)), ("fu", L, fb): wload(("fu", L, fb))}
                for j in range(2):
                    f = fb * 2 + j
                    psg, psgb = proj_fm(T, ("fg", L, fb), j, 16, rhs, rhsb, wc)
                    psu, psub = proj_fm(T, ("fu", L, fb), j, 16, rhs, rhsb, wc)
                    si = f % 2
                    S.act(sg_t[:, si, 0:T], psg[:, 0:T], ACT.Silu, [], [psgb, sg_b[si]])
                    S.tt("dve", RA.bf16(f * 1024, T), psu[:, 0:T], sg_t[:, si, 0:T], ALU.mult, [sg_b[si]],
                         [psub] + RA.bufs(f * 1024, f * 1024 + 2 * T))
            for m in range(16):
                ps, psb = mm_bank()
                for half in range(2):
                    wap, wb = wload(("fd", L, m, half))
                    for k in range(22):
                        f = half * 22 + k
                        S.mm(ps[:, 0:T], wap[:, k, :], RA.bf16(f * 1024, T), f == 0, f == NF - 1,
                             [wb] + RA.bufs(f * 1024, f * 1024 + 2 * T), [psb])
                evac_B_and_stats(T, m, ps, psb)
            post_norm_residual(T, "g_ffn_post", L)

        def ple(T, L, final=False):
            for c in range(16):
                S.copy("pool" if c % 2 == 0 else "act", hn_t[:, c, 0:T], h_t[:, c, 0:T], [h_b[c]], [hn_b[c]])
            rhs, rhsb = hn_rhs(T)
            for mb in range(8):
                wc = {("pg", L, mb): wload(("pg", L, mb))}
                wap, wb = wload(("pp", L, mb))
                for j in range(2):
                    m = mb * 2 + j
                    psg, psgb = proj_fm(T, ("pg", L, mb), j, 16, rhs, rhsb, wc)
                    psp, pspb = mm_bank()
                    for k in range(2):
                        S.mm(psp[:, 0:T], wap[:, k, j * 128:(j + 1) * 128], pbf[:, L, k, 0:T],
                             k == 0, k == 1, [wb, pbf_b], [pspb])
                    si = m % 2
                    S.act(sg_t[:, si, 0:T], psg[:, 0:T], ACT.Sigmoid, [], [psgb, sg_b[si]])
                    bb = RB.bufs(m * 2048, m * 2048 + 4 * T)
                    Bm = RB.f32(m * 2048, T)
                    S.tt("dve", Bm, psp[:, 0:T], sg_t[:, si, 0:T], ALU.mult, [sg_b[si]], [pspb] + bb)
                    sumsq_accumulate(T, Bm, bb, m, 16, "pool")
            post_norm_residual(T, "g_ple", L, out_to_B=final)

        UW = 30 + TT
        UB = UW * 2

        def conv_glu(T, prevT, is_halo):
            pre_norm(T, "g_mix_pre", 1)
            rhs, rhsb = hn_rhs(T)
            for m in range(16):
                ulo = m * UB
                ub_all = RA.bufs(ulo, ulo + UB)
                S.copy("pool", RA.bf16(ulo, 30), uh[:, m, 0:30], [uh_b[m]], ub_all)
                wc = {("pw1", m): wload(("pw1", m))}
                ps1, ps1b = proj_fm(T, ("pw1", m), 0, 16, rhs, rhsb, wc)
                ps2, ps2b = proj_fm(T, ("pw1", m), 1, 16, rhs, rhsb, wc)
                si = m % 2
                S.act(sg_t[:, si, 0:T], ps2[:, 0:T], ACT.Sigmoid, [vecs_b], [ps2b, sg_b[si]],
                      bias=V("b_pw1", 16 + m))
                S.stt("dve", RA.bf16(ulo + 60, T), ps1[:, 0:T], V("b_pw1", m), sg_t[:, si, 0:T], ALU.add, ALU.mult,
                      [sg_b[si], vecs_b], [ps1b] + ub_all)
                if is_halo:
                    S.ts("dve", RA.bf16(ulo + 60, T), RA.bf16(ulo + 60, T), V("halo"), None, ALU.mult, None,
                         ub_all + [vecs_b], ub_all)
                S.copy("pool", uh[:, m, 0:30], RA.bf16(ulo + 2 * T, 30), ub_all, [uh_b[m]])

        def conv_rest(T):
            for m in range(16):
                ulo = m * UB
                ub_all = RA.bufs(ulo, ulo + UB)
                bb = RB.bufs(m * 2048, m * 2048 + 4 * T)
                Bm = RB.f32(m * 2048, T)
                wap, wb = wload(("dw", m))
                ps, psb = mm_bank()
                for k in range(CW):
                    S.mm(ps[:, 0:T], wap[:, k, :], RA.bf16(ulo + 2 * k, T), k == 0, k == CW - 1, [wb] + ub_all, [psb])
                S.act(Bm, ps[:, 0:T], ACT.Identity, [vecs_b], [psb] + bb, bias=V("b_dw", m))
                i = sq_rr[0] % NSQ
                sq_rr[0] += 1
                S.copy("act", sq_t[:, i, 0:T], Bm, bb, [sq_b[i]])
                S.mm(ST2[:, 0:T], ones[:], sq_t[:, i, 0:T], m == 0, m == 15, [sq_b[i], const_b], [ST2b])
                sumsq_accumulate(T, Bm, bb, m, 16, "act")
            S.act(mean[:, 0:T], ST2[:, 0:T], ACT.Copy, [], [ST2b, mean_b], scale=1.0 / D)
            S.tt("dve", tmpA[:, 0:T], mean[:, 0:T], mean[:, 0:T], ALU.mult, [mean_b], [tmpA_b])
            S.stt("dve", tmpA[:, 0:T], ST[:, 0:T], 1.0 / D, tmpA[:, 0:T], ALU.mult, ALU.subtract, [tmpA_b],
                  [STb, tmpA_b])
            S.act(rstd[:, 0:T], tmpA[:, 0:T], ACT.Sqrt, [tmpA_b, vecs_b], [rstd_b], bias=V("eps"))
            S.op("dve", lambda e: e.reciprocal(out=rstd[:, 0:T], in_=rstd[:, 0:T]), [rstd_b], [rstd_b])
            for m in range(16):
                bb = RB.bufs(m * 2048, m * 2048 + 4 * T)
                Bm = RB.f32(m * 2048, T)
                eng = "dve" if m % 2 == 0 else "pool"
                S.tt(eng, Bm, Bm, mean[:, 0:T], ALU.subtract, bb + [mean_b], bb)
                S.tt(eng, Bm, Bm, rstd[:, 0:T], ALU.mult, bb + [rstd_b], bb)
                S.act(hn_t[:, m, 0:T], Bm, ACT.Silu, bb + [vecs_b], [hn_b[m]], bias=V("ln_b", m), scale=V("ln_g", m))
            rhs, rhsb = hn_rhs(T)
            for mb in range(8):
                wc = {("pw2", mb): wload(("pw2", mb))}
                for j in range(2):
                    m = mb * 2 + j
                    ps, psb = proj_fm(T, ("pw2", mb), j, 16, rhs, rhsb, wc)
                    evac_B_and_stats(T, m, ps, psb, bias=V("b_pw2", m))
            post_norm_residual(T, "g_mix_post", 1)

        ds_dbg = dsem("ds_dbg")
        final_ops = []

        def dump(stage, from_B):
            if not debug:
                return
            if from_B:
                src = RB.t[:, 0:16 * TT].rearrange("p (c t) -> p c t", c=16)
                bufs = RB.g
            else:
                src = h_t[:]
                bufs = h_b
            final_ops.append(S.dma("act", dbg[stage].rearrange("c p t -> p c t"), src, bufs, [], ds_dbg))

        tiles = [(0, 128)] + [(128 + i * TT, TT) for i in range(nmain)]

        def zero_state():
            S.op("pool", lambda e: e.memset(Ssl[:, 0], 0.0), [], Ssl_b[0])
            for hh in range(HEADS):
                S.dma("pool", Sd[hh], Ssl[:, 0].rearrange("p m v -> p (m v)"), Ssl_b[0], [Sd_b[hh]], ds_Ss[0])

        if "A" in phases:
            zero_state()
            for ti, (t0, T) in enumerate(tiles):
                last_chunk_flag[0] = (ti == len(tiles) - 1)
                load_tile_inputs(t0, T, False)
                trig_tables(T)
                pre_norm(T, "g_mix_pre", 0)
                retention_heads(T, True)
            last_chunk_flag[0] = False
            if not fused:
                final_ops.extend(Sd_b[hh].last_w for hh in range(HEADS))

        if fused:
            S.op("pool", lambda e: e.collective_compute(
                "AllGather", ALU.bypass, replica_groups=[list(range(8))],
                ins=[Sd.rearrange("h p v -> (h p) v").opt()], outs=[s_all.opt()]), Sd_b, [Sall_b], ds_cc, inc=1)

        if "P" in phases:
            zero_state()
            kv_prefix[0] = True
            ntp = NPRE // TT
            lg = [math.log1p(-2.0 ** (-5 - hh)) for hh in range(HEADS)]
            for t in range(ntp):
                dist = (ntp - 1 - t) * TT
                heads = [hh for hh in range(HEADS) if dist * (-lg[hh]) < 20.7]
                if not heads:
                    continue
                load_tile_inputs(t * TT, TT, False, xprev, posprev)
                trig_tables(TT)
                pre_norm(TT, "g_mix_pre", 0)
                retention_heads(TT, True, heads)
            kv_prefix[0] = False

        if "B" in phases:
            if "P" not in phases:
                for hh in range(HEADS):
                    par = hh % 2
                    S.op("dve", lambda e, a=Ssl[:, par]: e.memset(a, 0.0), [], Ssl_b[par])
                    for i in range(8):
                        bb = RB.bufs(0, 4096)
                        r0 = (i * HEADS + hh) * 128
                        S.dma("act", RB.f32(0, 2 * DV), s_all[r0:r0 + 128, :], [Sall_b], bb, ds_s)
                        S.stt("dve", Ssl[:, par].rearrange("p m v -> p (m v)"), RB.f32(0, 2 * DV),
                              V("coef", i * 8 + hh), Ssl[:, par].rearrange("p m v -> p (m v)"), ALU.mult, ALU.add,
                              bb + [vecs_b] + Ssl_b[par], Ssl_b[par])
                    S.dma("pool", Sd[hh], Ssl[:, par].rearrange("p m v -> p (m v)"), Ssl_b[par], [Sd_b[hh]],
                          ds_Ss[par])
            prevT = None
            for ti, (t0, T) in enumerate(tiles):
                is_halo = (ti == 0)
                load_tile_inputs(t0, T, True)
                trig_tables(T)
                pre_norm(T, "g_mix_pre", 0)
                retention_heads(T, False)
                mixer_out_proj(T)
                if ti == 1:
                    dump(0, True)
                post_norm_residual(T, "g_mix_post", 0)
                if ti == 1:
                    dump(1, False)
                ffn(T, 0)
                if ti == 1:
                    dump(2, False)
                ple(T, 0)
                if ti == 1:
                    dump(3, False)
                conv_glu(T, prevT, is_halo)
                prevT = T
                if is_halo:
                    continue
                conv_rest(T)
                if ti == 1:
                    dump(5, False)
                ffn(T, 1)
                if ti == 1:
                    dump(6, False)
                ple(T, 1, final=True)
                if ti == 1:
                    dump(7, True)
                o0 = t0 - 128
                final_ops.append(S.dma("act", outT[:, :, o0:o0 + T].rearrange("c p t -> p c t"),
                                       RB.t[:, 0:16 * TT].rearrange("p (c t) -> p c t", c=16)[:, :, 0:T], RB.g, [],
                                       ds_out))

        S.emit(es, "act", final_ops)
    return nc, S


def _core_inputs(inputs, nmain, wsrc, tabs):
    x = np.asarray(inputs["x"], np.float32)
    p = np.asarray(inputs["p"], np.float32)
    pos = np.asarray(inputs["positions"], np.int32)
    B, SEQ, _ = x.shape
    seg = nmain * TT
    nseg = SEQ // seg
    assert nseg == 4 and B == 2
    maps = []
    for c in range(8):
        b, j = c // nseg, c % nseg
        s0 = j * seg - 128
        NTOK = 128 + seg
        xs = np.zeros((NTOK, D), np.float32)
        ps = np.zeros((2, NTOK, PLE), np.float32)
        po = np.zeros((NTOK,), np.int32)
        lo = max(s0, 0)
        xs[lo - s0:] = x[b, lo:s0 + NTOK]
        ps[:, lo - s0:] = p[:, b, lo:s0 + NTOK]
        po[lo - s0:] = pos[b, lo:s0 + NTOK]
        NPRE = 3 * seg
        xp = np.zeros((NPRE, D), np.float32)
        pp = np.zeros((NPRE,), np.int32)
        npre = max(s0, 0)
        if npre > 0:
            xp[NPRE - npre:] = x[b, :npre]
            pp[NPRE - npre:] = pos[b, :npre]
        m = {
            "xprev": np.ascontiguousarray(xp.T.reshape(16, 128, NPRE)),
            "posprev": np.ascontiguousarray(np.broadcast_to(pp[None, :], (128, NPRE))),
            "xT": np.ascontiguousarray(xs.T.reshape(16, 128, NTOK)),
            "pT": np.ascontiguousarray(ps.transpose(0, 2, 1).reshape(2, 2, 128, NTOK)),
            "posb": np.ascontiguousarray(np.broadcast_to(po[None, :], (128, NTOK))),
            "wsrc": wsrc,
            "vecs": pack_vecs(inputs, j, seg, b),
            "tabs": tabs.reshape(128, -1),
        }
        maps.append(m)
    return maps


def run(inputs, nmain):
    inputs = {k: np.asarray(v) for k, v in inputs.items()}
    blks, WTOT = weight_blocks()
    wsrc = pack_weights(inputs, blks, WTOT)
    tabs = pack_tabs()
    maps = _core_inputs(inputs, nmain, wsrc, tabs)
    seg = nmain * TT
    ncB, _ = build_program(nmain, {"P", "B"})
    resB = run_bass_kernel_spmd(ncB, maps, core_ids=list(range(8)))
    B = 2
    out = np.empty((B, 4 * seg, D), np.float32)
    for c in range(8):
        b, j = c // 4, c % 4
        oT = np.asarray(resB.results[c]["outT"]).reshape(D, seg)
        out[b, j * seg:(j + 1) * seg] = oT.T
    return out


def kernel(**inputs):
    return run(inputs, 8)
```

```python
import math
from contextlib import ExitStack

import numpy as np
import concourse.bass as bass
import concourse.mybir as mybir
from concourse.bass_utils import run_bass_kernel_spmd

F32 = mybir.dt.float32
BF16 = mybir.dt.bfloat16
I32 = mybir.dt.int32
ACT = mybir.ActivationFunctionType
ALU = mybir.AluOpType

D = 2048
NC16 = 16
HEADS = 8
DK = 256
DV = 512
VW = 4096
FF = 5632
NF = 44
PLE = 256
CW = 31
EPS = 1e-6
TT = 512
NSLOT = 4
SLOT = 4096
PIECE = 32768
TWO_PI = 2.0 * math.pi
C1 = 6.28125
C2 = TWO_PI - C1


class Buf:
    __slots__ = ("last_w", "readers")

    def __init__(self):
        self.last_w = None
        self.readers = []


class DSem:
    __slots__ = ("handle", "count")

    def __init__(self, handle):
        self.handle = handle
        self.count = 0


class Op:
    __slots__ = ("eng", "fn", "deps", "signal", "tok", "dsem", "is_dma", "seq", "inc")

    def __init__(self, eng, fn, dsem, seq, inc=16):
        self.inc = inc
        self.eng = eng
        self.fn = fn
        self.deps = []
        self.signal = False
        self.tok = None
        self.dsem = dsem
        self.is_dma = dsem is not None
        self.seq = seq


class Sched:
    ENGS = ("pe", "act", "dve", "pool", "sp")
    ENGOBJ = {"pe": "tensor", "act": "scalar", "dve": "vector", "pool": "gpsimd", "sp": "sync"}

    def __init__(self, nc):
        self.nc = nc
        self.ops = {e: [] for e in self.ENGS}
        self.nops = 0
        self.nwaits = 0

    @staticmethod
    def _add_dep(deps, d):
        if not d.is_dma:
            for i, x in enumerate(deps):
                if (not x.is_dma) and x.eng == d.eng:
                    if d.seq > x.seq:
                        deps[i] = d
                    return
        else:
            for x in deps:
                if x is d:
                    return
        deps.append(d)

    def op(self, eng, fn, reads=(), writes=(), dsem=None, inc=16):
        o = Op(eng, fn, dsem, self.nops, inc)
        self.nops += 1
        deps = o.deps
        for b in reads:
            if b.last_w is not None:
                self._add_dep(deps, b.last_w)
        for b in writes:
            if b.last_w is not None:
                self._add_dep(deps, b.last_w)
            for r in b.readers:
                self._add_dep(deps, r)
        for b in reads:
            self._add_dep(b.readers, o)
        for b in writes:
            b.last_w = o
            b.readers = []
        self.ops[eng].append(o)
        return o

    def dma(self, q, out, in_, reads, writes, dsem):
        return self.op(q, lambda e: e.dma_start(out=out, in_=in_), reads, writes, dsem)

    def mm(self, out, lhsT, rhs, start, stop, reads, writes):
        return self.op("pe", lambda e: e.matmul(out, lhsT=lhsT, rhs=rhs, start=start, stop=stop), reads, writes)

    def tr(self, out, in_, ident, reads, writes):
        return self.op("pe", lambda e: e.transpose(out, in_, ident), reads, writes)

    def act(self, out, in_, func, reads, writes, bias=None, scale=None, accum_out=None):
        kw = {}
        if bias is not None:
            kw["bias"] = bias
        if scale is not None:
            kw["scale"] = scale
        if accum_out is not None:
            kw["accum_out"] = accum_out
        return self.op("act", lambda e: e.activation(out=out, in_=in_, func=func, **kw), reads, writes)

    def copy(self, eng, out, in_, reads, writes):
        if eng == "act":
            return self.act(out, in_, ACT.Copy, reads, writes)
        return self.op(eng, lambda e: e.tensor_copy(out=out, in_=in_), reads, writes)

    def tt(self, eng, out, in0, in1, op, reads, writes):
        return self.op(eng, lambda e: e.tensor_tensor(out=out, in0=in0, in1=in1, op=op), reads, writes)

    def ts(self, eng, out, in0, s1, s2, op0, op1, reads, writes):
        if op1 is None:
            return self.op(eng, lambda e: e.tensor_scalar(out=out, in0=in0, scalar1=s1, scalar2=None, op0=op0),
                           reads, writes)
        return self.op(eng, lambda e: e.tensor_scalar(out=out, in0=in0, scalar1=s1, scalar2=s2, op0=op0, op1=op1),
                       reads, writes)

    def stt(self, eng, out, in0, scalar, in1, op0, op1, reads, writes):
        return self.op(eng, lambda e: e.scalar_tensor_tensor(out=out, in0=in0, scalar=scalar, in1=in1,
                                                             op0=op0, op1=op1), reads, writes)

    def emit(self, es, final_eng, final_ops):
        nc = self.nc
        for e in self.ENGS:
            for o in self.ops[e]:
                nd = []
                for d in o.deps:
                    if (not d.is_dma) and (not o.is_dma) and d.eng == "pe" and o.eng == "pe":
                        continue
                    d.signal = True
                    nd.append(d)
                o.deps = nd
        for o in final_ops:
            o.signal = True
        for e in self.ENGS:
            cnt = 0
            for o in self.ops[e]:
                if o.is_dma:
                    o.dsem.count += o.inc
                    o.tok = (o.dsem.handle, o.dsem.count)
                elif o.signal:
                    cnt += 1
                    o.tok = (e, cnt)
        esem = {e: es.enter_context(nc.semaphore("es_" + e)) for e in ("pe", "act", "dve", "pool")}

        def run_engine(e, eng):
            waited = {}

            def wait_for(deps):
                need = {}
                for d in deps:
                    s, v = d.tok
                    if need.get(s, 0) < v:
                        need[s] = v
                for s, v in need.items():
                    if waited.get(s, 0) < v:
                        waited[s] = v
                        eng.wait_ge(esem[s] if isinstance(s, str) else s, v)
                        self.nwaits += 1

            for o in self.ops[e]:
                wait_for(o.deps)
                ins = o.fn(eng)
                if o.is_dma:
                    ins.then_inc(o.dsem.handle, o.inc)
                elif o.signal:
                    ins.then_inc(esem[e], 1)
            if e == final_eng:
                wait_for(final_ops)

        with nc.Block() as block:
            for e in self.ENGS:
                if not self.ops[e] and e != final_eng:
                    continue
                getattr(block, self.ENGOBJ[e])(lambda eng, e=e: run_engine(e, eng))


class Region:
    def __init__(self, nc, es, name, nbytes, gran=1024):
        assert nbytes % 4 == 0
        self.t = es.enter_context(nc.sbuf_tensor(name, [128, nbytes // 4], F32))
        self.nbytes = nbytes
        self.gran = gran
        self.g = [Buf() for _ in range((nbytes + gran - 1) // gran)]

    def bufs(self, lo, hi):
        return self.g[lo // self.gran:(hi - 1) // self.gran + 1]

    def f32(self, lo, n):
        return self.t[:, lo // 4: lo // 4 + n]

    def bf16(self, lo, n):
        e0 = lo // 2
        f0 = e0 // 2
        f1 = (e0 + n + 1) // 2
        return self.t[:, f0:f1].bitcast(BF16)[:, e0 - 2 * f0: e0 - 2 * f0 + n]


def weight_blocks():
    blks = []

    def add(key, src, li, r0, kc, cols):
        n = sum(c[1] for c in cols)
        blks.append(dict(key=key, src=src, li=li, r0=r0, kc=kc, cols=cols, ncols=n, nel=kc * n))

    for hh in reversed(range(HEADS)):
        add(("pk", hh), "ret_w_in", 0, 0, 16, [(D + hh * DK, DK)])
        for half in range(2):
            add(("pv", hh, half), "ret_w_in", 0, half * 1024, 8, [(2 * D + hh * DV, DV)])
    for hh in range(HEADS):
        add(("q", hh), "ret_w_in", 0, 0, 16, [(hh * DK, DK)])
        add(("k", hh), "ret_w_in", 0, 0, 16, [(D + hh * DK, DK)])
        for half in range(2):
            add(("v", hh, half), "ret_w_in", 0, half * 1024, 8, [(2 * D + hh * DV, DV)])
        for half in range(2):
            add(("g", hh, half), "ret_w_in", 0, half * 1024, 8, [(2 * D + VW + hh * DV, DV)])
    for m in range(NC16):
        add(("wo", m), "ret_w_out", 0, 0, 32, [(m * 128, 128)])
    for L in range(2):
        if L == 1:
            for m in range(NC16):
                add(("pw1", m), "conv_w_pw1", 0, 0, 16, [(m * 128, 128), (D + m * 128, 128)])
            for m in range(NC16):
                add(("dw", m), "dwdiag", m, 0, CW, [(0, 128)])
            for mb in range(8):
                add(("pw2", mb), "conv_w_pw2", 0, 0, 16, [(mb * 256, 256)])
        for fb in range(NF // 2):
            add(("fg", L, fb), "ffn_w_gate", L, 0, 16, [(fb * 256, 256)])
            add(("fu", L, fb), "ffn_w_up", L, 0, 16, [(fb * 256, 256)])
        for m in range(NC16):
            for half in range(2):
                add(("fd", L, m, half), "ffn_w_down", L, half * 22 * 128, 22, [(m * 128, 128)])
        for mb in range(8):
            add(("pg", L, mb), "ple_w_gate", L, 0, 16, [(mb * 256, 256)])
            add(("pp", L, mb), "ple_w_proj", L, 0, 2, [(mb * 256, 256)])

    off = 0
    for b in blks:
        b["off"] = off
        off += b["nel"]
    return blks, off


def cast_pieces(blks):
    pieces = []
    cur0, cur = 0, 0
    for b in blks:
        cap = 8192 if len(pieces) < 8 else PIECE
        if cur + b["nel"] > cap and cur > 0:
            pieces.append((cur0, cur))
            cur0 += cur
            cur = 0
        b["piece"] = len(pieces)
        cur += b["nel"]
    pieces.append((cur0, cur))
    return pieces


def pack_weights(inputs, blks, total):
    ws = np.empty((128, total), np.float32)
    for b in blks:
        if b["src"] == "dwdiag":
            m = b["li"]
            wd = inputs["conv_w_dw"][0][:, m * 128:(m + 1) * 128]
            sub = np.zeros((128, CW, 128), np.float32)
            ar = np.arange(128)
            sub[ar, :, ar] = wd.T
            ws[:, b["off"]: b["off"] + b["nel"]] = sub.reshape(128, b["nel"])
            continue
        W = inputs[b["src"]][b["li"]]
        rows = W[b["r0"]: b["r0"] + b["kc"] * 128]
        sub = np.concatenate([rows[:, c0:c0 + n] for (c0, n) in b["cols"]], axis=1)
        sub = sub.reshape(b["kc"], 128, b["ncols"]).transpose(1, 0, 2).reshape(128, b["nel"])
        ws[:, b["off"]: b["off"] + b["nel"]] = sub
    return ws


def vec_layout():
    lay = {}
    off = 0

    def add(name, n):
        nonlocal off
        lay[name] = off
        off += n

    for L in range(2):
        for nm in ("g_mix_pre", "g_mix_post", "g_ffn_pre", "g_ffn_post", "g_ple"):
            add((nm, L), 16)
    add("b_pw1", 32)
    add("b_dw", 16)
    add("ln_g", 16)
    add("ln_b", 16)
    add("b_pw2", 16)
    add("w_dw", 16 * CW)
    add("inv_freq", 1)
    add("halo", 1)
    add("kdec", 8)
    add("g1", 8)
    add("g2s", 8)
    add("coef", 64)
    add("eps", 1)
    return lay, off


def chunked(v):
    return np.asarray(v, np.float32).reshape(-1, 128).T


def pack_vecs(inputs, core_j, seg, core_b=0, nseg=4):
    lay, n = vec_layout()
    V = np.zeros((128, n), np.float32)
    for L in range(2):
        for nm in ("g_mix_pre", "g_mix_post", "g_ffn_pre", "g_ffn_post", "g_ple"):
            V[:, lay[(nm, L)]: lay[(nm, L)] + 16] = chunked(inputs[nm][L])
    V[:, lay["b_pw1"]: lay["b_pw1"] + 32] = chunked(inputs["conv_b_pw1"][0])
    V[:, lay["b_dw"]: lay["b_dw"] + 16] = chunked(inputs["conv_b_dw"][0])
    V[:, lay["ln_g"]: lay["ln_g"] + 16] = chunked(inputs["conv_ln_g"][0])
    V[:, lay["ln_b"]: lay["ln_b"] + 16] = chunked(inputs["conv_ln_b"][0])
    V[:, lay["b_pw2"]: lay["b_pw2"] + 16] = chunked(inputs["conv_b_pw2"][0])
    wd = inputs["conv_w_dw"][0]
    for m in range(16):
        V[:, lay["w_dw"] + m * CW: lay["w_dw"] + (m + 1) * CW] = wd[:, m * 128:(m + 1) * 128].T
    half = 128
    inv_freq = (np.float32(10000.0) ** (-np.arange(half, dtype=np.float32) / np.float32(half))).astype(np.float32)
    V[:, lay["inv_freq"]] = inv_freq
    V[:, lay["halo"]] = 0.0 if core_j == 0 else 1.0
    lg = np.log1p(-np.exp2(-5.0 - np.arange(HEADS, dtype=np.float32))).astype(np.float32)
    idx = np.arange(128, dtype=np.float32)
    V[:, lay["kdec"]: lay["kdec"] + 8] = np.exp(lg[None, :] * (127.0 - idx)[:, None])
    lg64 = lg.astype(np.float64)
    V[:, lay["g1"]: lay["g1"] + 8] = np.exp(lg64[None, :] * (idx.astype(np.float64) + 1.0)[:, None])
    V[:, lay["g2s"]: lay["g2s"] + 8] = np.exp(2.0 * lg64[None, :] * (idx.astype(np.float64) + 1.0)[:, None]) / DV
    for r in range(8):
        rb, i = r // nseg, r % nseg
        for hh in range(HEADS):
            c = 0.0
            if rb == core_b and i < core_j:
                c = math.exp(float(np.float64(lg[hh])) * seg * (core_j - 1 - i))
            V[:, lay["coef"] + r * 8 + hh] = c
    V[:, lay["eps"]] = EPS
    return V


def pack_tabs():
    lg = np.log1p(-np.exp2(-5.0 - np.arange(HEADS, dtype=np.float32))).astype(np.float64)
    idx = np.arange(128, dtype=np.float64)
    T = np.zeros((128, 1, HEADS, 128), np.float32)
    for hh in range(HEADS):
        causal = (idx[None, :] >= idx[:, None]).astype(np.float64)
        T[:, 0, hh, :] = np.exp(-lg[hh] * (idx[:, None] + 1.0)) * causal
    return T


def state_decay():
    lg = np.log1p(-np.exp2(-5.0 - np.arange(HEADS, dtype=np.float32))).astype(np.float64)
    return [float(np.exp(lg[h] * 128.0)) for h in range(HEADS)]


def build_program(nmain, phases, debug=False):
    NTOK = 128 + nmain * TT
    blks, WTOT = weight_blocks()
    pieces = cast_pieces(blks)
    bidx = {b["key"]: b for b in blks}
    lay, NV = vec_layout()
    sdec = state_decay()
    fused = False
    NPRE = 3 * nmain * TT

    nc = bass.Bass("TRN2", target_bir_lowering=False)
    xT = nc.dram_tensor("xT", [16, 128, NTOK], F32, kind="ExternalInput").ap()
    pT = nc.dram_tensor("pT", [2, 2, 128, NTOK], F32, kind="ExternalInput").ap()
    posb = nc.dram_tensor("posb", [128, NTOK], I32, kind="ExternalInput").ap()
    if "P" in phases:
        xprev = nc.dram_tensor("xprev", [16, 128, NPRE], F32, kind="ExternalInput").ap()
        posprev = nc.dram_tensor("posprev", [128, NPRE], I32, kind="ExternalInput").ap()
    wsrc = nc.dram_tensor("wsrc", [128, WTOT], F32, kind="ExternalInput").ap()
    vecs_d = nc.dram_tensor("vecs", [128, NV], F32, kind="ExternalInput").ap()
    tabs_d = nc.dram_tensor("tabs", [128, HEADS * 128], F32, kind="ExternalInput").ap()
    WSPLIT = [p0 for (p0, pn) in pieces if p0 >= WTOT // 2][0]
    wbfA = nc.dram_tensor("wbfA", [128, WSPLIT], BF16, kind="Internal").ap()
    wbfB = nc.dram_tensor("wbfB", [128, WTOT - WSPLIT], BF16, kind="Internal").ap()

    def wbf_slice(o0, n):
        if o0 >= WSPLIT:
            return wbfB[:, o0 - WSPLIT:o0 - WSPLIT + n]
        assert o0 + n <= WSPLIT
        return wbfA[:, o0:o0 + n]

    if "A" in phases and not fused:
        Sd = nc.dram_tensor("s_loc", [HEADS, 128, 2 * DV], F32, kind="ExternalOutput").ap()
    else:
        Sd = nc.dram_tensor("Sd", [HEADS, 128, 2 * DV], F32, kind="Internal").ap()
    if "B" in phases and "P" not in phases:
        s_all = nc.dram_tensor("s_all", [8 * HEADS * 128, 2 * DV], F32, kind="ExternalInput").ap()
    if fused:
        s_all = nc.dram_tensor("Sall", [8 * HEADS * 128, 2 * DV], F32).ap()
    if "B" in phases:
        outT = nc.dram_tensor("outT", [16, 128, nmain * TT], F32, kind="ExternalOutput").ap()

    if debug:
        dbg = nc.dram_tensor("dbg", [8, 16, 128, TT], F32, kind="ExternalOutput").ap()
    S = Sched(nc)
    with ExitStack() as es:
        def sbuf(name, shape, dt):
            return es.enter_context(nc.sbuf_tensor(name, shape, dt))

        def dsem(name):
            return DSem(es.enter_context(nc.semaphore(name)))

        h_t = sbuf("h", [128, 16, TT], F32)
        h_b = [Buf() for _ in range(16)]
        hn_t = sbuf("hn", [128, 16, TT], BF16)
        hn_b = [Buf() for _ in range(16)]
        RA = Region(nc, es, "RA", NF * 1024)
        RB = Region(nc, es, "RB", 32 * 1024)
        Ssl = sbuf("Ssl", [128, 2, 2, DV], F32)
        Ssl_b = [[Buf(), Buf()], [Buf(), Buf()]]
        Sd_b = [Buf() for _ in range(HEADS)]
        Sbf_t = sbuf("Sbf", [128, 2, 2, DV], BF16)
        Sbf_b = [[Buf(), Buf()], [Buf(), Buf()]]
        wr_t = sbuf("wring", [128, NSLOT, SLOT], BF16)
        wr_b = [Buf() for _ in range(NSLOT)]
        wr_sem = [dsem(f"wr{i}") for i in range(NSLOT)]
        vecs = sbuf("vecs_sb", [128, NV], F32)
        vecs_b = Buf()
        tabs = sbuf("tabs_sb", [128, 1, HEADS, 128], F32)
        tabs_b = Buf()
        ident = sbuf("ident", [128, 128], BF16)
        identf = sbuf("identf", [128, 128], F32)
        ones = sbuf("ones", [128, 128], BF16)
        const_b = Buf()
        posi = sbuf("posi", [128, TT], I32)
        posi_b = Buf()
        cosT = sbuf("cosT", [128, TT], F32)
        sinT = sbuf("sinT", [128, TT], F32)
        trig_b = Buf()
        tmpA = sbuf("tmpA", [128, TT], F32)
        tmpA_b = Buf()
        tmpB = sbuf("tmpB", [128, TT], F32)
        tmpB_b = Buf()
        tmpI = sbuf("tmpI", [128, TT], I32)
        tmpI_b = Buf()
        rstd = sbuf("rstd", [128, TT], F32)
        rstd_b = Buf()
        mean = sbuf("mean", [128, TT], F32)
        mean_b = Buf()
        NSQ = 2
        sq_t = sbuf("sq", [128, NSQ, TT], BF16)
        sq_b = [Buf() for _ in range(NSQ)]
        sg_t = sbuf("sg", [128, 2, TT], F32)
        sg_b = [Buf(), Buf()]
        pbf = sbuf("pbf", [128, 2, 2, TT], BF16)
        pbf_b = Buf()
        small = sbuf("small", [128, 8], F32)
        small_b = [Buf() for _ in range(6)]
        junk = sbuf("junk", [128, DV], BF16)
        junk_b = Buf()
        Pm = sbuf("Pm", [128, 2, 128], BF16)
        Pm_b = [Buf(), Buf()]
        ytok = sbuf("ytok", [128, 2, DV], BF16)
        ytok_b = [Buf(), Buf()]
        uh = sbuf("uh", [128, 16, 30], BF16)
        uh_b = [Buf() for _ in range(16)]

        banks = [es.enter_context(nc.psum_tensor(f"pb{i}", [128, 512], F32)) for i in range(8)]
        bank_b = [Buf() for _ in range(8)]
        rr = {"mm": 0, "ob": 0}

        def mm_bank():
            i = rr["mm"] % 4
            rr["mm"] += 1
            return banks[i], bank_b[i]

        def ob_bank():
            i = 6 + rr["ob"] % 2
            rr["ob"] += 1
            return banks[i], bank_b[i]

        ST, STb = banks[4], bank_b[4]
        TRf, TRb = banks[5], bank_b[5]
        TR = TRf[:].bitcast(BF16)
        ST2, ST2b = banks[6], bank_b[6]

        ds_const = dsem("ds_const")
        ds_x = dsem("ds_x")
        ds_p = dsem("ds_p")
        ds_pos = dsem("ds_pos")
        ds_out = dsem("ds_out")
        ds_s = dsem("ds_s")
        ds_cc = dsem("ds_cc")
        Sall_b = Buf()
        ds_Sl = [dsem("ds_Sl0"), dsem("ds_Sl1")]
        ds_Ss = [dsem("ds_Ss0"), dsem("ds_Ss1")]
        ds_cast = [dsem(f"ds_cast{i}") for i in range(len(pieces))]

        def V(name, c=0, n=1):
            o = lay[name] + c
            return vecs[:, o:o + n]

        S.dma("act", vecs[:], vecs_d, [], [vecs_b], ds_const)
        S.dma("act", tabs[:].rearrange("p a h i -> p (a h i)"), tabs_d, [], [tabs_b], ds_const)
        S.op("pool", lambda e: e.memset(identf[:], 0.0), [], [const_b])
        S.op("pool", lambda e: e.affine_select(out=identf[:], in_=identf[:], compare_op=ALU.not_equal, fill=1.0,
                                                base=0, pattern=[[-1, 128]], channel_multiplier=1),
             [const_b], [const_b])
        S.copy("dve", ident[:], identf[:], [const_b], [const_b])
        S.op("dve", lambda e: e.memset(ones[:], 1.0), [], [const_b])
        S.op("pool", lambda e: e.memset(uh[:], 0.0), [], uh_b)

        piece_b = [Buf() for _ in pieces]
        need_keys = None
        if phases == {"A"}:
            need_keys = set()
            for hh in range(HEADS):
                need_keys |= {("k", hh), ("v", hh, 0), ("v", hh, 1)}
        need_pieces = set(range(len(pieces))) if need_keys is None else {bidx[k]["piece"] for k in need_keys}
        pending_casts = [pi for pi in range(len(pieces)) if pi in need_pieces]

        def issue_casts(n):
            for _ in range(n):
                if not pending_casts:
                    return
                pi = pending_casts.pop(0)
                p0, pn = pieces[pi]
                S.dma("pool", wbf_slice(p0, pn), wsrc[:, p0:p0 + pn], [], [piece_b[pi]], ds_cast[pi])

        n_first = 1 + max(b["piece"] for b in blks if b["key"][0] in ("pk", "pv"))
        issue_casts(n_first if "P" in phases else len(pending_casts))

        slot_rr = [0]

        def wload(key):
            b = bidx[key]
            s = slot_rr[0] % NSLOT
            slot_rr[0] += 1
            assert piece_b[b["piece"]].last_w is not None, "weight block used before its cast was issued"
            dst = wr_t[:, s, 0:b["nel"]]
            S.dma("sp", dst, wbf_slice(b["off"], b["nel"]), [piece_b[b["piece"]]], [wr_b[s]], wr_sem[s])
            return dst.rearrange("p (k c) -> p k c", k=b["kc"]), wr_b[s]

        def load_tile_inputs(t0, T, with_p, xsrc=None, psrc=None):
            xsrc = xT if xsrc is None else xsrc
            psrc = posb if psrc is None else psrc
            S.dma("act", h_t[:, :, 0:T], xsrc[:, :, t0:t0 + T].rearrange("c p t -> p c t"), [], h_b, ds_x)
            S.dma("act", posi[:, 0:T], psrc[:, t0:t0 + T], [], [posi_b], ds_pos)
            if with_p:
                S.dma("pool", pbf[:, :, :, 0:T], pT[:, :, :, t0:t0 + T].rearrange("l k p t -> p l k t"), [], [pbf_b],
                      ds_p)

        def trig_tables(T):
            S.copy("dve", tmpA[:, 0:T], posi[:, 0:T], [posi_b], [tmpA_b])
            S.ts("dve", tmpA[:, 0:T], tmpA[:, 0:T], V("inv_freq"), None, ALU.mult, None, [tmpA_b, vecs_b], [tmpA_b])
            for (dst, phase) in ((sinT, 0.0), (cosT, 0.25)):
                S.ts("dve", tmpI[:, 0:T], tmpA[:, 0:T], 1.0 / TWO_PI, phase, ALU.mult, ALU.add, [tmpA_b], [tmpI_b])
                S.copy("dve", tmpB[:, 0:T], tmpI[:, 0:T], [tmpI_b], [tmpB_b])
                S.stt("dve", rstd[:, 0:T], tmpB[:, 0:T], -C1, tmpA[:, 0:T], ALU.mult, ALU.add,
                      [tmpB_b, tmpA_b], [rstd_b])
                S.stt("dve", rstd[:, 0:T], tmpB[:, 0:T], -C2, rstd[:, 0:T], ALU.mult, ALU.add,
                      [tmpB_b, rstd_b], [rstd_b])
                if phase != 0.0:
                    S.ts("dve", rstd[:, 0:T], rstd[:, 0:T], phase * TWO_PI, None, ALU.add, None, [rstd_b], [rstd_b])
                S.ts("dve", rstd[:, 0:T], rstd[:, 0:T], -math.pi, math.pi, ALU.max, ALU.min, [rstd_b], [rstd_b])
                S.act(dst[:, 0:T], rstd[:, 0:T], ACT.Sin, [rstd_b], [trig_b])

        def stats_finish_rstd(T, ps, psb, scale, dst, dst_b):
            S.act(dst[:, 0:T], ps[:, 0:T], ACT.Sqrt, [vecs_b], [psb, dst_b], bias=V("eps"), scale=scale)
            S.op("dve", lambda e: e.reciprocal(out=dst[:, 0:T], in_=dst[:, 0:T]), [dst_b], [dst_b])

        sq_rr = [0]

        def sumsq_accumulate(T, src_ap, src_bufs, c, n, eng="act"):
            i = sq_rr[0] % NSQ
            sq_rr[0] += 1
            if eng == "act":
                S.act(sq_t[:, i, 0:T], src_ap, ACT.Square, src_bufs, [sq_b[i]])
            else:
                S.tt(eng, sq_t[:, i, 0:T], src_ap, src_ap, ALU.mult, src_bufs, [sq_b[i]])
            S.mm(ST[:, 0:T], ones[:], sq_t[:, i, 0:T], c == 0, c == n - 1, [sq_b[i], const_b], [STb])

        def pre_norm(T, gname, L):
            for c in range(16):
                sumsq_accumulate(T, h_t[:, c, 0:T], [h_b[c]], c, 16, "act" if c % 2 == 0 else "pool")
            stats_finish_rstd(T, ST, STb, 1.0 / D, rstd, rstd_b)
            for c in range(16):
                eng = "dve"
                S.stt(eng, hn_t[:, c, 0:T], h_t[:, c, 0:T], V((gname, L), c), rstd[:, 0:T], ALU.mult, ALU.mult,
                      [h_b[c], rstd_b, vecs_b], [hn_b[c]])

        def post_norm_residual(T, gname, L, out_to_B=False):
            stats_finish_rstd(T, ST, STb, 1.0 / D, rstd, rstd_b)
            for c in range(16):
                eng = "dve" if c % 2 == 0 else "pool"
                bb = RB.bufs(c * 2048, c * 2048 + 4 * T)
                Bc = RB.f32(c * 2048, T)
                S.stt("dve", Bc, Bc, V((gname, L), c), rstd[:, 0:T], ALU.mult, ALU.mult, bb + [rstd_b, vecs_b], bb)
                if out_to_B:
                    S.tt("pool", Bc, h_t[:, c, 0:T], Bc, ALU.add, bb + [h_b[c]], bb)
                else:
                    S.tt("pool", h_t[:, c, 0:T], h_t[:, c, 0:T], Bc, ALU.add, bb + [h_b[c]], [h_b[c]])

        def evac_B_and_stats(T, m, ps, psb, bias=None):
            bb = RB.bufs(m * 2048, m * 2048 + 4 * T)
            Bm = RB.f32(m * 2048, T)
            if bias is None:
                S.act(Bm, ps[:, 0:T], ACT.Copy, [], [psb] + bb)
            else:
                S.act(Bm, ps[:, 0:T], ACT.Identity, [vecs_b], [psb] + bb, bias=bias)
            sumsq_accumulate(T, Bm, bb, m, 16, "pool" if m % 2 == 0 else "dve")

        def proj_fm(T, key, m_local, kc, rhs_fn, rhs_bufs_fn, wcache):
            wap, wb = wcache[key]
            ps, psb = mm_bank()
            for k in range(kc):
                S.mm(ps[:, 0:T], wap[:, k, m_local * 128:(m_local + 1) * 128], rhs_fn(k), k == 0, k == kc - 1,
                     [wb] + rhs_bufs_fn(k), [psb])
            return ps, psb

        def hn_rhs(T):
            return (lambda k: hn_t[:, k, 0:T]), (lambda k: [hn_b[k]])

        def head_views(T, par):
            base = par * 14336
            o = {}
            o["qf"] = (base, 2 * 4 * T)
            o["qd"] = (base + 4096, 2 * 2 * T)
            o["kT"] = (base + 6144, 2 * 2 * T)
            o["ktok"] = (base + 8192, 2 * T * 2)
            o["v"] = (base + 10240, 2 * T * 4)
            return o

        def rotary(T, src_lo, dst_lo, eng_pair=("dve", "pool")):
            x1 = RB.f32(src_lo, T)
            x2 = RB.f32(src_lo + 4 * T, T)
            xb = RB.bufs(src_lo, src_lo + 8 * T)
            e0, e1 = eng_pair
            S.tt(e0, tmpA[:, 0:T], x1, cosT[:, 0:T], ALU.mult, xb + [trig_b], [tmpA_b])
            S.tt(e0, tmpB[:, 0:T], x2, sinT[:, 0:T], ALU.mult, xb + [trig_b], [tmpB_b])
            S.tt(e0, mean[:, 0:T], x1, sinT[:, 0:T], ALU.mult, xb + [trig_b], [mean_b])
            S.tt(e0, x2, x2, cosT[:, 0:T], ALU.mult, xb + [trig_b], xb)
            db = RB.bufs(dst_lo, dst_lo + 4 * T)
            S.tt(e0, RB.bf16(dst_lo, T), tmpA[:, 0:T], tmpB[:, 0:T], ALU.subtract, [tmpA_b, tmpB_b], db)
            S.tt(e0, RB.bf16(dst_lo + 2 * T, T), mean[:, 0:T], x2, ALU.add, [mean_b] + xb, db)

        def head_bufs(T, hh):
            par = hh % 2
            hv = head_views(T, par)
            o = dict(par=par)
            o["qf_lo"] = hv["qf"][0]
            o["qfb"] = RB.bufs(o["qf_lo"], o["qf_lo"] + 8 * T)
            o["qd_lo"] = hv["qd"][0]
            o["qdb"] = RB.bufs(o["qd_lo"], o["qd_lo"] + 4 * T)
            o["kT_lo"] = hv["kT"][0]
            o["kTb"] = RB.bufs(o["kT_lo"], o["kT_lo"] + 4 * T)
            o["kt_lo"] = hv["ktok"][0]
            o["ktb"] = RB.bufs(o["kt_lo"], o["kt_lo"] + 4 * T)
            o["v_lo"] = hv["v"][0]
            o["vb"] = RB.bufs(o["v_lo"], o["v_lo"] + 8 * T)
            o["sg_lo"] = 32768 + par * 4096
            o["sgb"] = RA.bufs(o["sg_lo"], o["sg_lo"] + 1024 * (T // 128))
            return o

        kv_prefix = [False]

        def head_proj(T, hh, state_only):
            nch = T // 128
            hb = head_bufs(T, hh)
            par = hb["par"]
            rhs, rhsb = hn_rhs(T)
            qf_lo, qfb, qd_lo, kT_lo, kTb = hb["qf_lo"], hb["qfb"], hb["qd_lo"], hb["kT_lo"], hb["kTb"]
            kt_lo, ktb, v_lo, vb = hb["kt_lo"], hb["ktb"], hb["v_lo"], hb["vb"]
            S.dma("pool", Ssl[:, par].rearrange("p m v -> p (m v)"), Sd[hh], [Sd_b[hh]], Ssl_b[par], ds_Sl[par])
            wc = {}
            if not state_only:
                wc[("q", hh)] = wload(("q", hh))
                for m in range(2):
                    ps, psb = proj_fm(T, ("q", hh), m, 16, rhs, rhsb, wc)
                    S.act(RB.f32(qf_lo + 4 * T * m, T), ps[:, 0:T], ACT.Copy, [], [psb] + qfb)
                    yield
                rotary(T, qf_lo, qd_lo)
            kkey = ("pk", hh) if kv_prefix[0] else ("k", hh)
            wc[kkey] = wload(kkey)
            for m in range(2):
                ps, psb = proj_fm(T, kkey, m, 16, rhs, rhsb, wc)
                S.act(RB.f32(qf_lo + 4 * T * m, T), ps[:, 0:T], ACT.Copy, [], [psb] + qfb, scale=DK ** -0.5)
                yield
            rotary(T, qf_lo, kT_lo)
            vps = [mm_bank() for _ in range(nch)]
            for half in range(2):
                wap, wb = wload(("pv", hh, half) if kv_prefix[0] else ("v", hh, half))
                for c in range(nch):
                    ps, psb = vps[c]
                    for k in range(8):
                        kk = half * 8 + k
                        S.mm(ps[:, 0:DV], hn_t[:, kk, c * 128:(c + 1) * 128], wap[:, k, :],
                             kk == 0, kk == 15, [wb, hn_b[kk]], [psb])
                    yield
            for c in range(nch):
                ps, psb = vps[c]
                S.copy("act", RB.bf16(v_lo + 1024 * c, DV), ps[:, 0:DV], [], [psb] + vb)
            if not state_only:
                gps = [mm_bank() for _ in range(nch)]
                for half in range(2):
                    wap, wb = wload(("g", hh, half))
                    for c in range(nch):
                        ps, psb = gps[c]
                        for k in range(8):
                            kk = half * 8 + k
                            S.mm(ps[:, 0:DV], hn_t[:, kk, c * 128:(c + 1) * 128], wap[:, k, :],
                                 kk == 0, kk == 15, [wb, hn_b[kk]], [psb])
                        yield
                for c in range(nch):
                    ps, psb = gps[c]
                    S.act(RA.bf16(hb["sg_lo"] + 1024 * c, DV), ps[:, 0:DV], ACT.Silu, [], [psb] + hb["sgb"])
            for c in range(nch):
                for m in range(2):
                    S.tr(TR[:, (c * 2 + m) * 128:(c * 2 + m + 1) * 128],
                         RB.bf16(kT_lo + 2 * T * m + 256 * c, 128), ident[:], kTb + [const_b], [TRb])
            S.act(RB.bf16(kt_lo, 2 * T), TR[:, 0:2 * T], ACT.Copy, [vecs_b], [TRb] + ktb, scale=V("kdec", hh))
            yield

        def head_chunks(T, hh, state_only, skip_last):
            nch = T // 128
            hb = head_bufs(T, hh)
            par = hb["par"]
            qd_lo, qdb, kT_lo, kTb = hb["qd_lo"], hb["qdb"], hb["kT_lo"], hb["kTb"]
            kt_lo, ktb, v_lo, vb, sg_lo, sgb = hb["kt_lo"], hb["ktb"], hb["v_lo"], hb["vb"], hb["sg_lo"], hb["sgb"]
            for c in range(nch):
                vc = RB.bf16(v_lo + 1024 * c, DV)
                ktc = [RB.bf16(kt_lo + 512 * c + 256 * m, 128) for m in range(2)]
                if not state_only:
                    qdc = [RB.bf16(qd_lo + 2 * T * m + 256 * c, 128) for m in range(2)]
                    kTc = [RB.bf16(kT_lo + 2 * T * m + 256 * c, 128) for m in range(2)]
                    for m in range(2):
                        S.copy("act", Sbf_t[:, par, m, :], Ssl[:, par, m, :], [Ssl_b[par][m]], [Sbf_b[par][m]])
                    S.mm(ST[:, 0:128], kTc[0], qdc[0], True, False, kTb + qdb, [STb])
                    S.mm(ST[:, 0:128], kTc[1], qdc[1], False, True, kTb + qdb, [STb])
                    S.tt("dve", Pm[:, par, :], ST[:, 0:128], tabs[:, 0, hh, :], ALU.mult, [tabs_b],
                         [STb, Pm_b[par]])
                    yield
                    ops_, opb = ob_bank()
                    S.mm(ops_[:, 0:DV], Pm[:, par, :], vc, True, False, [Pm_b[par]] + vb, [opb])
                    S.mm(ops_[:, 0:DV], qdc[0], Sbf_t[:, par, 0, :], False, False, qdb + [Sbf_b[par][0]], [opb])
                    S.mm(ops_[:, 0:DV], qdc[1], Sbf_t[:, par, 1, :], False, True, qdb + [Sbf_b[par][1]], [opb])
                    si = par
                    S.act(junk[:], ops_[:, 0:DV], ACT.Square, [], [opb, junk_b, small_b[si]],
                          accum_out=small[:, si:si + 1])
                    S.act(small[:, 2 + si:3 + si], small[:, si:si + 1], ACT.Sqrt, [vecs_b, small_b[si]],
                          [small_b[2 + si]], bias=V("eps"), scale=V("g2s", hh))
                    S.op("dve", lambda e, a=small[:, 2 + si:3 + si]: e.reciprocal(out=a, in_=a),
                         [small_b[2 + si]], [small_b[2 + si]])
                    S.tt("dve", small[:, 4 + si:5 + si], small[:, 2 + si:3 + si], V("g1", hh), ALU.mult,
                         [small_b[2 + si], vecs_b], [small_b[4 + si]])
                    S.stt("dve", ytok[:, par, :], ops_[:, 0:DV], small[:, 4 + si:5 + si],
                          RA.bf16(sg_lo + 1024 * c, DV), ALU.mult, ALU.mult,
                          [small_b[4 + si]] + sgb, [opb, ytok_b[par]])
                    yield
                    for vcn in range(4):
                        S.tr(TR[:, vcn * 128:(vcn + 1) * 128], ytok[:, par, vcn * 128:(vcn + 1) * 128], ident[:],
                             [ytok_b[par], const_b], [TRb])
                    for vcn in range(4):
                        f = hh * 4 + vcn
                        lo = f * 1024 + 256 * c
                        S.copy("act" if vcn % 2 == 0 else "dve", RA.bf16(lo, 128),
                               TR[:, vcn * 128:(vcn + 1) * 128], [], [TRb] + RA.bufs(lo, lo + 256))
                    yield
                if not (skip_last and c == nch - 1):
                    for m in range(2):
                        ps, psb = ob_bank()
                        S.mm(ps[:, 0:DV], ktc[m], vc, True, True, ktb + vb, [psb])
                        S.stt("dve", Ssl[:, par, m, :], Ssl[:, par, m, :], sdec[hh], ps[:, 0:DV],
                              ALU.mult, ALU.add, [], [psb, Ssl_b[par][m]])
                yield
            S.dma("pool", Sd[hh], Ssl[:, par].rearrange("p m v -> p (m v)"), Ssl_b[par], [Sd_b[hh]], ds_Ss[par])

        def retention_heads(T, state_only, heads=None):
            hl = list(range(HEADS) if heads is None else heads)
            skip_last = state_only and last_chunk_flag[0]
            prev = None
            for hh in hl:
                pg = head_proj(T, hh, state_only)
                if prev is None:
                    for _ in pg:
                        pass
                else:
                    a_done = b_done = False
                    while not (a_done and b_done):
                        if not a_done:
                            try:
                                next(pg)
                            except StopIteration:
                                a_done = True
                        if not b_done:
                            try:
                                next(prev)
                            except StopIteration:
                                b_done = True
                prev = head_chunks(T, hh, state_only, skip_last)
            for _ in prev:
                pass

        last_chunk_flag = [False]

        def mixer_out_proj(T):
            for m in range(16):
                wc = {("wo", m): wload(("wo", m))}
                ps, psb = proj_fm(T, ("wo", m), 0, 32, lambda k: RA.bf16(k * 1024, T),
                                  lambda k: RA.bufs(k * 1024, k * 1024 + 2 * T), wc)
                evac_B_and_stats(T, m, ps, psb)

        def ffn(T, L):
            pre_norm(T, "g_ffn_pre", L)
            rhs, rhsb = hn_rhs(T)
            for fb in range(NF // 2):
                wc = {("fg", L, fb): wload(("fg", L, fb)), ("fu", L, fb): wload(("fu", L, fb))}
                for j in range(2):
                    f = fb * 2 + j
                    psg, psgb = proj_fm(T, ("fg", L, fb), j, 16, rhs, rhsb, wc)
                    psu, psub = proj_fm(T, ("fu", L, fb), j, 16, rhs, rhsb, wc)
                    si = f % 2
                    S.act(sg_t[:, si, 0:T], psg[:, 0:T], ACT.Silu, [], [psgb, sg_b[si]])
                    S.tt("dve", RA.bf16(f * 1024, T), psu[:, 0:T], sg_t[:, si, 0:T], ALU.mult, [sg_b[si]],
                         [psub] + RA.bufs(f * 1024, f * 1024 + 2 * T))
            for m in range(16):
                ps, psb = mm_bank()
                for half in range(2):
                    wap, wb = wload(("fd", L, m, half))
                    for k in range(22):
                        f = half * 22 + k
                        S.mm(ps[:, 0:T], wap[:, k, :], RA.bf16(f * 1024, T), f == 0, f == NF - 1,
                             [wb] + RA.bufs(f * 1024, f * 1024 + 2 * T), [psb])
                evac_B_and_stats(T, m, ps, psb)
            post_norm_residual(T, "g_ffn_post", L)

        def ple(T, L, final=False):
            for c in range(16):
                S.copy("pool" if c % 2 == 0 else "act", hn_t[:, c, 0:T], h_t[:, c, 0:T], [h_b[c]], [hn_b[c]])
            rhs, rhsb = hn_rhs(T)
            for mb in range(8):
                wc = {("pg", L, mb): wload(("pg", L, mb))}
                wap, wb = wload(("pp", L, mb))
                for j in range(2):
                    m = mb * 2 + j
                    psg, psgb = proj_fm(T, ("pg", L, mb), j, 16, rhs, rhsb, wc)
                    psp, pspb = mm_bank()
                    for k in range(2):
                        S.mm(psp[:, 0:T], wap[:, k, j * 128:(j + 1) * 128], pbf[:, L, k, 0:T],
                             k == 0, k == 1, [wb, pbf_b], [pspb])
                    si = m % 2
                    S.act(sg_t[:, si, 0:T], psg[:, 0:T], ACT.Sigmoid, [], [psgb, sg_b[si]])
                    bb = RB.bufs(m * 2048, m * 2048 + 4 * T)
                    Bm = RB.f32(m * 2048, T)
                    S.tt("dve", Bm, psp[:, 0:T], sg_t[:, si, 0:T], ALU.mult, [sg_b[si]], [pspb] + bb)
                    sumsq_accumulate(T, Bm, bb, m, 16, "pool")
            post_norm_residual(T, "g_ple", L, out_to_B=final)

        UW = 30 + TT
        UB = UW * 2

        def conv_glu(T, prevT, is_halo):
            pre_norm(T, "g_mix_pre", 1)
            rhs, rhsb = hn_rhs(T)
            for m in range(16):
                ulo = m * UB
                ub_all = RA.bufs(ulo, ulo + UB)
                S.copy("pool", RA.bf16(ulo, 30), uh[:, m, 0:30], [uh_b[m]], ub_all)
                wc = {("pw1", m): wload(("pw1", m))}
                ps1, ps1b = proj_fm(T, ("pw1", m), 0, 16, rhs, rhsb, wc)
                ps2, ps2b = proj_fm(T, ("pw1", m), 1, 16, rhs, rhsb, wc)
                si = m % 2
                S.act(sg_t[:, si, 0:T], ps2[:, 0:T], ACT.Sigmoid, [vecs_b], [ps2b, sg_b[si]],
                      bias=V("b_pw1", 16 + m))
                S.stt("dve", RA.bf16(ulo + 60, T), ps1[:, 0:T], V("b_pw1", m), sg_t[:, si, 0:T], ALU.add, ALU.mult,
                      [sg_b[si], vecs_b], [ps1b] + ub_all)
                if is_halo:
                    S.ts("dve", RA.bf16(ulo + 60, T), RA.bf16(ulo + 60, T), V("halo"), None, ALU.mult, None,
                         ub_all + [vecs_b], ub_all)
                S.copy("pool", uh[:, m, 0:30], RA.bf16(ulo + 2 * T, 30), ub_all, [uh_b[m]])

        def conv_rest(T):
            for m in range(16):
                ulo = m * UB
                ub_all = RA.bufs(ulo, ulo + UB)
                bb = RB.bufs(m * 2048, m * 2048 + 4 * T)
                Bm = RB.f32(m * 2048, T)
                wap, wb = wload(("dw", m))
                ps, psb = mm_bank()
                for k in range(CW):
                    S.mm(ps[:, 0:T], wap[:, k, :], RA.bf16(ulo + 2 * k, T), k == 0, k == CW - 1, [wb] + ub_all, [psb])
                S.act(Bm, ps[:, 0:T], ACT.Identity, [vecs_b], [psb] + bb, bias=V("b_dw", m))
                i = sq_rr[0] % NSQ
                sq_rr[0] += 1
                S.copy("act", sq_t[:, i, 0:T], Bm, bb, [sq_b[i]])
                S.mm(ST2[:, 0:T], ones[:], sq_t[:, i, 0:T], m == 0, m == 15, [sq_b[i], const_b], [ST2b])
                sumsq_accumulate(T, Bm, bb, m, 16, "act")
            S.act(mean[:, 0:T], ST2[:, 0:T], ACT.Copy, [], [ST2b, mean_b], scale=1.0 / D)
            S.tt("dve", tmpA[:, 0:T], mean[:, 0:T], mean[:, 0:T], ALU.mult, [mean_b], [tmpA_b])
            S.stt("dve", tmpA[:, 0:T], ST[:, 0:T], 1.0 / D, tmpA[:, 0:T], ALU.mult, ALU.subtract, [tmpA_b],
                  [STb, tmpA_b])
            S.act(rstd[:, 0:T], tmpA[:, 0:T], ACT.Sqrt, [tmpA_b, vecs_b], [rstd_b], bias=V("eps"))
            S.op("dve", lambda e: e.reciprocal(out=rstd[:, 0:T], in_=rstd[:, 0:T]), [rstd_b], [rstd_b])
            for m in range(16):
                bb = RB.bufs(m * 2048, m * 2048 + 4 * T)
                Bm = RB.f32(m * 2048, T)
                eng = "dve" if m % 2 == 0 else "pool"
                S.tt(eng, Bm, Bm, mean[:, 0:T], ALU.subtract, bb + [mean_b], bb)
                S.tt(eng, Bm, Bm, rstd[:, 0:T], ALU.mult, bb + [rstd_b], bb)
                S.act(hn_t[:, m, 0:T], Bm, ACT.Silu, bb + [vecs_b], [hn_b[m]], bias=V("ln_b", m), scale=V("ln_g", m))
            rhs, rhsb = hn_rhs(T)
            for mb in range(8):
                wc = {("pw2", mb): wload(("pw2", mb))}
                for j in range(2):
                    m = mb * 2 + j
                    ps, psb = proj_fm(T, ("pw2", mb), j, 16, rhs, rhsb, wc)
                    evac_B_and_stats(T, m, ps, psb, bias=V("b_pw2", m))
            post_norm_residual(T, "g_mix_post", 1)

        ds_dbg = dsem("ds_dbg")
        final_ops = []

        def dump(stage, from_B):
            if not debug:
                return
            if from_B:
                src = RB.t[:, 0:16 * TT].rearrange("p (c t) -> p c t", c=16)
                bufs = RB.g
            else:
                src = h_t[:]
                bufs = h_b
            final_ops.append(S.dma("act", dbg[stage].rearrange("c p t -> p c t"), src, bufs, [], ds_dbg))

        tiles = [(0, 128)] + [(128 + i * TT, TT) for i in range(nmain)]

        def zero_state():
            S.op("pool", lambda e: e.memset(Ssl[:, 0], 0.0), [], Ssl_b[0])
            for hh in range(HEADS):
                S.dma("pool", Sd[hh], Ssl[:, 0].rearrange("p m v -> p (m v)"), Ssl_b[0], [Sd_b[hh]], ds_Ss[0])

        if "A" in phases:
            zero_state()
            for ti, (t0, T) in enumerate(tiles):
                last_chunk_flag[0] = (ti == len(tiles) - 1)
                load_tile_inputs(t0, T, False)
                trig_tables(T)
                pre_norm(T, "g_mix_pre", 0)
                retention_heads(T, True)
            last_chunk_flag[0] = False
            if not fused:
                final_ops.extend(Sd_b[hh].last_w for hh in range(HEADS))

        if fused:
            S.op("pool", lambda e: e.collective_compute(
                "AllGather", ALU.bypass, replica_groups=[list(range(8))],
                ins=[Sd.rearrange("h p v -> (h p) v").opt()], outs=[s_all.opt()]), Sd_b, [Sall_b], ds_cc, inc=1)

        if "P" in phases:
            zero_state()
            kv_prefix[0] = True
            ntp = NPRE // TT
            lg = [math.log1p(-2.0 ** (-5 - hh)) for hh in range(HEADS)]
            for t in range(ntp):
                dist = (ntp - 1 - t) * TT
                heads = [hh for hh in range(HEADS) if dist * (-lg[hh]) < 20.7]
                if not heads:
                    continue
                load_tile_inputs(t * TT, TT, False, xprev, posprev)
                trig_tables(TT)
                pre_norm(TT, "g_mix_pre", 0)
                retention_heads(TT, True, heads)
                issue_casts(3)
            kv_prefix[0] = False
            issue_casts(len(pending_casts))

        if "B" in phases:
            if "P" not in phases:
                for hh in range(HEADS):
                    par = hh % 2
                    S.op("dve", lambda e, a=Ssl[:, par]: e.memset(a, 0.0), [], Ssl_b[par])
                    for i in range(8):
                        bb = RB.bufs(0, 4096)
                        r0 = (i * HEADS + hh) * 128
                        S.dma("act", RB.f32(0, 2 * DV), s_all[r0:r0 + 128, :], [Sall_b], bb, ds_s)
                        S.stt("dve", Ssl[:, par].rearrange("p m v -> p (m v)"), RB.f32(0, 2 * DV),
                              V("coef", i * 8 + hh), Ssl[:, par].rearrange("p m v -> p (m v)"), ALU.mult, ALU.add,
                              bb + [vecs_b] + Ssl_b[par], Ssl_b[par])
                    S.dma("pool", Sd[hh], Ssl[:, par].rearrange("p m v -> p (m v)"), Ssl_b[par], [Sd_b[hh]],
                          ds_Ss[par])
            prevT = None
            for ti, (t0, T) in enumerate(tiles):
                is_halo = (ti == 0)
                load_tile_inputs(t0, T, True)
                trig_tables(T)
                pre_norm(T, "g_mix_pre", 0)
                retention_heads(T, False)
                mixer_out_proj(T)
                if ti == 1:
                    dump(0, True)
                post_norm_residual(T, "g_mix_post", 0)
                if ti == 1:
                    dump(1, False)
                ffn(T, 0)
                if ti == 1:
                    dump(2, False)
                ple(T, 0)
                if ti == 1:
                    dump(3, False)
                conv_glu(T, prevT, is_halo)
                prevT = T
                if is_halo:
                    continue
                conv_rest(T)
                if ti == 1:
                    dump(5, False)
                ffn(T, 1)
                if ti == 1:
                    dump(6, False)
                ple(T, 1, final=True)
                if ti == 1:
                    dump(7, True)
                o0 = t0 - 128
                final_ops.append(S.dma("act", outT[:, :, o0:o0 + T].rearrange("c p t -> p c t"),
                                       RB.t[:, 0:16 * TT].rearrange("p (c t) -> p c t", c=16)[:, :, 0:T], RB.g, [],
                                       ds_out))

        S.emit(es, "act", final_ops)
    return nc, S


def _core_inputs(inputs, nmain, wsrc, tabs):
    x = np.asarray(inputs["x"], np.float32)
    p = np.asarray(inputs["p"], np.float32)
    pos = np.asarray(inputs["positions"], np.int32)
    B, SEQ, _ = x.shape
    seg = nmain * TT
    nseg = SEQ // seg
    assert nseg == 4 and B == 2
    maps = []
    for c in range(8):
        b, j = c // nseg, c % nseg
        s0 = j * seg - 128
        NTOK = 128 + seg
        xs = np.zeros((NTOK, D), np.float32)
        ps = np.zeros((2, NTOK, PLE), np.float32)
        po = np.zeros((NTOK,), np.int32)
        lo = max(s0, 0)
        xs[lo - s0:] = x[b, lo:s0 + NTOK]
        ps[:, lo - s0:] = p[:, b, lo:s0 + NTOK]
        po[lo - s0:] = pos[b, lo:s0 + NTOK]
        NPRE = 3 * seg
        xp = np.zeros((NPRE, D), np.float32)
        pp = np.zeros((NPRE,), np.int32)
        npre = max(s0, 0)
        if npre > 0:
            xp[NPRE - npre:] = x[b, :npre]
            pp[NPRE - npre:] = pos[b, :npre]
        m = {
            "xprev": np.ascontiguousarray(xp.T.reshape(16, 128, NPRE)),
            "posprev": np.ascontiguousarray(np.broadcast_to(pp[None, :], (128, NPRE))),
            "xT": np.ascontiguousarray(xs.T.reshape(16, 128, NTOK)),
            "pT": np.ascontiguousarray(ps.transpose(0, 2, 1).reshape(2, 2, 128, NTOK)),
            "posb": np.ascontiguousarray(np.broadcast_to(po[None, :], (128, NTOK))),
            "wsrc": wsrc,
            "vecs": pack_vecs(inputs, j, seg, b),
            "tabs": tabs.reshape(128, -1),
        }
        maps.append(m)
    return maps


def run(inputs, nmain):
    inputs = {k: np.asarray(v) for k, v in inputs.items()}
    blks, WTOT = weight_blocks()
    wsrc = pack_weights(inputs, blks, WTOT)
    tabs = pack_tabs()
    maps = _core_inputs(inputs, nmain, wsrc, tabs)
    seg = nmain * TT
    ncB, _ = build_program(nmain, {"P", "B"})
    resB = run_bass_kernel_spmd(ncB, maps, core_ids=list(range(8)))
    B = 2
    out = np.empty((B, 4 * seg, D), np.float32)
    for c in range(8):
        b, j = c // 4, c % 4
        oT = np.asarray(resB.results[c]["outT"]).reshape(D, seg)
        out[b, j * seg:(j + 1) * seg] = oT.T
    return out


def kernel(**inputs):
    return run(inputs, 8)
```
